# Optimizing a Trainium2 kernel written in Bass

```python
import jax, jax.numpy as jnp
from jax import lax
import numpy as np

D_MODEL = 1024
BATCH = 8
SEQ = 2048
DEPTH = 4
DEC_BATCH = 2
DEC_SEQ = 16384
PAST_LEN = 128

N_MIXERS = 2
EXPAND = 2
D_INNER = EXPAND * D_MODEL
N_RWKV_LAYERS = (DEPTH + N_MIXERS - 1) // N_MIXERS
N_MLA_LAYERS = DEPTH // N_MIXERS

RWKV_HEAD = 64
RWKV_HEADS = D_INNER // RWKV_HEAD
DECAY_LORA = 64
ICL_LORA = 64
LN_X_EPS = 64e-5
RWKV_COL_STREAMS = ((0, D_INNER), (2, D_INNER), (3, D_INNER), (5, D_INNER),
                    (1, 2 * DECAY_LORA), (4, 2 * ICL_LORA))
RWKV_IN_COLS = 4 * D_INNER + 2 * DECAY_LORA + 2 * ICL_LORA

MLA_HEADS = 16
QK_NOPE = 128
QK_ROPE = 64
V_HEAD = 128
QK_HEAD = QK_NOPE + QK_ROPE
Q_LORA = 384
KV_LORA = 256
ROPE_THETA = 10000.0
Q_BLOCK = 128
MLA_IN_COLS = Q_LORA + KV_LORA + QK_ROPE + D_INNER

NORM_EPS = 1e-6

kernel_name = "rwkv7_mla_interleaved_bidir_encoder"


def rmsnorm(x, g):
    xf = x.astype(jnp.float32)
    y = xf * lax.rsqrt(jnp.mean(xf * xf, axis=-1, keepdims=True) + NORM_EPS) * g.astype(jnp.float32)
    return y.astype(x.dtype)


def _wkv_step(S, inp):
    r, w, k, v, kk, b = inp
    sa = jnp.einsum('bhvk,bhk->bhv', S, -kk)
    S = S * w[:, :, None, :] + sa[..., None] * b[:, :, None, :] + v[..., None] * k[:, :, None, :]
    return S, jnp.einsum('bhvk,bhk->bhv', S, r)


def rwkv_mixer(h, mu, w_in, w0, w2, a0, a2, k_k, k_a, r_k, lnx_g, lnx_b, w_out):
    B, T, _ = h.shape
    prev = jnp.pad(h[:, :-1], ((0, 0), (1, 0), (0, 0)))
    nxt = jnp.pad(h[:, 1:], ((0, 0), (0, 1), (0, 0)))
    xx = 0.5 * (prev + nxt) - h
    mu_cols = jnp.concatenate(
        [jnp.broadcast_to(mu[s][:, None], (D_MODEL, wd)) for s, wd in RWKV_COL_STREAMS], axis=1)
    w_eff = jnp.concatenate([w_in, mu_cols * w_in], axis=0)
    proj = (jnp.concatenate([h, xx], axis=-1) @ w_eff).astype(jnp.float32)
    E = D_INNER
    r, k, v, g, wl, al = jnp.split(proj, [E, 2 * E, 3 * E, 4 * E, 4 * E + 2 * DECAY_LORA], axis=-1)
    wl = wl.reshape(B, T, 2, DECAY_LORA)
    al = al.reshape(B, T, 2, ICL_LORA)
    w_pre = w0.astype(jnp.float32) + jnp.einsum('btdl,dle->btde', jnp.tanh(wl), w2.astype(jnp.float32))
    decay = jnp.exp(-jnp.exp(-jax.nn.softplus(-w_pre) - 0.5))
    a = jax.nn.sigmoid(a0.astype(jnp.float32) + jnp.einsum('btdl,dle->btde', al, a2.astype(jnp.float32)))
    kk = (k * k_k).reshape(B, T, RWKV_HEADS, RWKV_HEAD)
    kk = kk / jnp.maximum(jnp.sqrt(jnp.sum(kk * kk, axis=-1, keepdims=True)), 1e-12)
    kk = kk.reshape(B, T, E)
    k_dir = k[:, :, None, :] * (1.0 + (a - 1.0) * k_a)
    b_dir = kk[:, :, None, :] * a

    def tm(z):
        return z.reshape(B, T, RWKV_HEADS, RWKV_HEAD).transpose(1, 0, 2, 3)

    r_t, v_t, kk_t = tm(r), tm(v), tm(kk)
    S0 = jnp.zeros((B, RWKV_HEADS, RWKV_HEAD, RWKV_HEAD), jnp.float32)
    _, y_f = lax.scan(_wkv_step, S0, (r_t, tm(decay[:, :, 0]), tm(k_dir[:, :, 0]), v_t, kk_t, tm(b_dir[:, :, 0])))
    _, y_b = lax.scan(_wkv_step, S0, (r_t, tm(decay[:, :, 1]), tm(k_dir[:, :, 1]), v_t, kk_t, tm(b_dir[:, :, 1])),
                      reverse=True)
    y = (y_f + y_b).transpose(1, 0, 2, 3)
    mean = jnp.mean(y, axis=-1, keepdims=True)
    var = jnp.mean(jnp.square(y - mean), axis=-1, keepdims=True)
    y = ((y - mean) * lax.rsqrt(var + LN_X_EPS)).reshape(B, T, E) * lnx_g + lnx_b
    rk = (r[:, :, None, :] * k_dir).reshape(B, T, 2, RWKV_HEADS, RWKV_HEAD) * r_k.astype(jnp.float32)
    bonus = jnp.sum(rk, axis=(2, 4))[..., None] * v.reshape(B, T, RWKV_HEADS, RWKV_HEAD)
    out = (y + bonus.reshape(B, T, E)) * jax.nn.silu(g)
    return (out @ w_out).astype(h.dtype)


def _rope(x, cos, sin):
    xf = x.astype(jnp.float32)
    x1, x2 = jnp.split(xf, 2, axis=-1)
    return jnp.concatenate([x1 * cos - x2 * sin, x2 * cos + x1 * sin], axis=-1).astype(x.dtype)


def mla_mixer(h, w_in, q_norm_g, kv_norm_g, w_uq, w_ukv, w_out):
    B, T, _ = h.shape
    proj = h @ w_in
    c_q, c_kv, k_pe, g = jnp.split(proj, [Q_LORA, Q_LORA + KV_LORA, Q_LORA + KV_LORA + QK_ROPE], axis=-1)
    q = (rmsnorm(c_q, q_norm_g) @ w_uq).reshape(B, T, MLA_HEADS, QK_HEAD)
    kv = (rmsnorm(c_kv, kv_norm_g) @ w_ukv).reshape(B, T, MLA_HEADS, QK_NOPE + V_HEAD)
    q_nope, q_pe = jnp.split(q, [QK_NOPE], axis=-1)
    k_nope, v = jnp.split(kv, [QK_NOPE], axis=-1)
    inv_freq = 1.0 / (ROPE_THETA ** (jnp.arange(0, QK_ROPE, 2, dtype=jnp.float32) / QK_ROPE))
    ang = jnp.arange(T, dtype=jnp.float32)[:, None] * inv_freq[None, :]
    cos, sin = jnp.cos(ang)[:, None, :], jnp.sin(ang)[:, None, :]
    q_pe = _rope(q_pe, cos, sin)
    k_pe = _rope(k_pe[:, :, None, :], cos, sin)
    q = jnp.concatenate([q_nope, q_pe], axis=-1) * (QK_HEAD ** -0.5)
    k = jnp.concatenate([k_nope, jnp.broadcast_to(k_pe, (B, T, MLA_HEADS, QK_ROPE))], axis=-1)
    nb = T // Q_BLOCK
    qb = q.reshape(B, nb, Q_BLOCK, MLA_HEADS, QK_HEAD).transpose(1, 0, 2, 3, 4)

    def attend(q_blk):
        s = jnp.einsum('bqhd,bkhd->bhqk', q_blk, k, preferred_element_type=jnp.float32)
        p = jax.nn.softmax(s, axis=-1).astype(v.dtype)
        return jnp.einsum('bhqk,bkhd->bqhd', p, v)

    o = lax.map(attend, qb)
    o = o.transpose(1, 0, 2, 3, 4).reshape(B, T, MLA_HEADS * V_HEAD)
    return ((o * jax.nn.silu(g)) @ w_out).astype(h.dtype)


def trunk(x, ln_g, final_g, rw_mu, rw_in, rw_w0, rw_w2, rw_a0, rw_a2, rw_kk, rw_ka, rw_rk,
          rw_lnx_g, rw_lnx_b, rw_out, ml_in, ml_qn, ml_kvn, ml_uq, ml_ukv, ml_out):
    for i in range(DEPTH):
        h = rmsnorm(x, ln_g[i])
        j = i // N_MIXERS
        if i % N_MIXERS == 0:
            x = x + rwkv_mixer(h, rw_mu[j], rw_in[j], rw_w0[j], rw_w2[j], rw_a0[j], rw_a2[j],
                               rw_kk[j], rw_ka[j], rw_rk[j], rw_lnx_g[j], rw_lnx_b[j], rw_out[j])
        else:
            x = x + mla_mixer(h, ml_in[j], ml_qn[j], ml_kvn[j], ml_uq[j], ml_ukv[j], ml_out[j])
    return rmsnorm(x, final_g)


def setup_inputs(seed: int = 0) -> dict:
    key = jax.random.key(seed)
    ks = jax.random.split(key, 24)
    f32 = jnp.float32

    def nrm(k, shape, s):
        return jax.random.normal(k, shape, f32) * s

    NA, NB, E = N_RWKV_LAYERS, N_MLA_LAYERS, D_INNER
    return {
        "x_prompt": nrm(ks[0], (BATCH, SEQ, D_MODEL), 1.0),
        "x_sample": nrm(ks[1], (DEC_BATCH, DEC_SEQ, D_MODEL), 1.0),
        "ln_g": 1.0 + nrm(ks[2], (DEPTH, D_MODEL), 0.02),
        "final_g": 1.0 + nrm(ks[3], (D_MODEL,), 0.02),
        "rw_mu": jax.random.uniform(ks[4], (NA, 6, D_MODEL), f32),
        "rw_in": nrm(ks[5], (NA, D_MODEL, RWKV_IN_COLS), D_MODEL ** -0.5),
        "rw_w0": jax.random.uniform(ks[6], (NA, 2, E), f32, minval=-6.5, maxval=-1.0),
        "rw_w2": nrm(ks[7], (NA, 2, DECAY_LORA, E), 0.1 * DECAY_LORA ** -0.5),
        "rw_a0": nrm(ks[8], (NA, 2, E), 0.1),
        "rw_a2": nrm(ks[9], (NA, 2, ICL_LORA, E), 0.5 * ICL_LORA ** -0.5),
        "rw_kk": 0.85 + nrm(ks[10], (NA, E), 0.05),
        "rw_ka": 1.0 + nrm(ks[11], (NA, E), 0.05),
        "rw_rk": nrm(ks[12], (NA, RWKV_HEADS, RWKV_HEAD), 0.05),
        "rw_lnx_g": 1.0 + nrm(ks[13], (NA, E), 0.02),
        "rw_lnx_b": nrm(ks[14], (NA, E), 0.02),
        "rw_out": nrm(ks[15], (NA, E, D_MODEL), E ** -0.5),
        "ml_in": nrm(ks[16], (NB, D_MODEL, MLA_IN_COLS), D_MODEL ** -0.5),
        "ml_qn": 1.0 + nrm(ks[17], (NB, Q_LORA), 0.02),
        "ml_kvn": 1.0 + nrm(ks[18], (NB, KV_LORA), 0.02),
        "ml_uq": nrm(ks[19], (NB, Q_LORA, MLA_HEADS * QK_HEAD), Q_LORA ** -0.5),
        "ml_ukv": nrm(ks[20], (NB, KV_LORA, MLA_HEADS * (QK_NOPE + V_HEAD)), KV_LORA ** -0.5),
        "ml_out": nrm(ks[21], (NB, E, D_MODEL), E ** -0.5),
    }


def reference(x_prompt, x_sample, ln_g, final_g, rw_mu, rw_in, rw_w0, rw_w2, rw_a0, rw_a2,
              rw_kk, rw_ka, rw_rk, rw_lnx_g, rw_lnx_b, rw_out, ml_in, ml_qn, ml_kvn, ml_uq,
              ml_ukv, ml_out):
    y_prompt = trunk(x_prompt, ln_g, final_g, rw_mu, rw_in, rw_w0, rw_w2, rw_a0, rw_a2, rw_kk, rw_ka,
                     rw_rk, rw_lnx_g, rw_lnx_b, rw_out, ml_in, ml_qn, ml_kvn, ml_uq, ml_ukv, ml_out)
    y_sample = trunk(x_sample, ln_g, final_g, rw_mu, rw_in, rw_w0, rw_w2, rw_a0, rw_a2, rw_kk, rw_ka,
                     rw_rk, rw_lnx_g, rw_lnx_b, rw_out, ml_in, ml_qn, ml_kvn, ml_uq, ml_ukv, ml_out)
    return (y_prompt, y_sample)
```

```python
import numpy as np
import ml_dtypes
import concourse.bass as bass
import concourse.mybir as mybir
from concourse.bass_utils import run_bass_kernel_spmd
from contextlib import ExitStack

F32 = mybir.dt.float32
BF16 = mybir.dt.bfloat16
AF = mybir.ActivationFunctionType
ALU = mybir.AluOpType
AX = mybir.AxisListType
class Prog:
    ENG = ('pe', 'act', 'dve', 'pool', 'sp')

    def __init__(self, nc):
        self.nc = nc
        self.streams = {e: [] for e in self.ENG}
        self.sems = {}
        self.cnt = {}
        self.waited = {e: {} for e in self.ENG}
        self.last_w = {}
        self.readers = {}
        self.nops = 0

    def sem(self, key):
        if key not in self.sems:
            self.sems[key] = self.nc.alloc_semaphore("s_" + key.replace(':', '_'))
            self.cnt[key] = 0
        return self.sems[key]

    def _need(self, eng, toks):
        best = {}
        w = self.waited[eng]
        for t in toks:
            if t is None:
                continue
            k, v = t
            if w.get(k, 0) >= v:
                continue
            if best.get(k, 0) < v:
                best[k] = v
        for k, v in best.items():
            w[k] = v
        return list(best.items())

    def _deps(self, reads, writes):
        toks = []
        for k in reads:
            toks.append(self.last_w.get(k))
        for k in writes:
            toks.append(self.last_w.get(k))
            r = self.readers.get(k)
            if r:
                toks.extend(r.items())
        return toks

    def _commit(self, tok, reads, writes):
        for k in reads:
            r = self.readers.setdefault(k, {})
            if r.get(tok[0], 0) < tok[1]:
                r[tok[0]] = tok[1]
        for k in writes:
            self.last_w[k] = tok
            self.readers[k] = {}

    def op(self, eng, fn, reads=(), writes=()):
        semkey = 'E:' + eng
        self.sem(semkey)
        toks = self._deps(reads, writes)
        if eng == 'pe':
            toks = [t for t in toks if t is not None and t[0] != semkey]
        waits = self._need(eng, toks)
        self.cnt[semkey] += 1
        tok = (semkey, self.cnt[semkey])
        self.streams[eng].append((waits, fn, semkey, 1))
        self._commit(tok, reads, writes)
        self.nops += 1
        return tok

    def dma(self, q, fn, slot, reads=(), writes=(), inc=16):
        semkey = 'D:' + slot
        self.sem(semkey)
        toks = self._deps(reads, writes)
        waits = self._need(q, toks)
        self.cnt[semkey] += inc
        tok = (semkey, self.cnt[semkey])
        self.streams[q].append((waits, fn, semkey, inc))
        self._commit(tok, reads, writes)
        self.nops += 1
        return tok

    def barrier(self):
        toks = [(k, v) for k, v in self.cnt.items() if v > 0]
        for e in self.ENG:
            waits = self._need(e, toks)
            if waits:
                self.streams[e].append((waits, None, None, 0))
        self.last_w.clear()
        self.readers.clear()

    def emit(self):
        nc = self.nc
        self.barrier()
        with nc.Block() as block:
            def run(h, name):
                sems = self.sems
                for waits, fn, semkey, inc in self.streams[name]:
                    for k, v in waits:
                        h.wait_ge(sems[k], v)
                    if fn is not None:
                        fn(h).then_inc(sems[semkey], inc)

            @block.tensor
            def _(h):
                run(h, 'pe')

            @block.scalar
            def _(h):
                run(h, 'act')

            @block.vector
            def _(h):
                run(h, 'dve')

            @block.gpsimd
            def _(h):
                run(h, 'pool')

            @block.sync
            def _(h):
                run(h, 'sp')

    def mm(self, out, lhsT, rhs, start=True, stop=True, r=(), w=()):
        return self.op('pe', lambda h: h.matmul(out, lhsT=lhsT, rhs=rhs, start=start, stop=stop), r, w)

    def tr(self, out, in_, ident, r=(), w=()):
        return self.op('pe', lambda h: h.transpose(out, in_, ident), r, w)

    def act(self, out, in_, func, bias=None, scale=1.0, accum=None, r=(), w=()):
        kw = {}
        if bias is not None:
            kw['bias'] = bias
        if accum is not None:
            kw['accum_out'] = accum
        return self.op('act', lambda h: h.activation(out, in_, func, scale=scale, **kw), r, w)

    def tt(self, eng, out, in0, in1, op, r=(), w=()):
        return self.op(eng, lambda h: h.tensor_tensor(out, in0, in1, op), r, w)

    def ts(self, eng, out, in0, s1, s2, op0, op1=None, r=(), w=()):
        if op1 is None:
            if op0 == ALU.mult:
                return self.op(eng, lambda h: h.tensor_scalar_mul(out, in0, s1), r, w)
            if op0 == ALU.add:
                return self.op(eng, lambda h: h.tensor_scalar_add(out, in0, s1), r, w)
            if op0 == ALU.max:
                return self.op(eng, lambda h: h.tensor_scalar_max(out, in0, s1), r, w)
            raise ValueError(op0)
        return self.op(eng, lambda h: h.tensor_scalar(out, in0, s1, s2, op0, op1), r, w)

    def stt(self, eng, out, in0, scalar, in1, op0, op1, r=(), w=()):
        return self.op(eng, lambda h: h.scalar_tensor_tensor(out, in0, scalar, in1, op0, op1), r, w)

    def cp(self, eng, out, in_, r=(), w=()):
        if eng == 'act':
            return self.op('act', lambda h: h.copy(out, in_), r, w)
        return self.op(eng, lambda h: h.tensor_copy(out, in_), r, w)

    def memset(self, eng, ap, val, w=()):
        return self.op(eng, lambda h: h.memset(ap, val), (), w)

    def load(self, out, in_, slot, r=(), w=(), q='sp', slow=False):
        if slow:
            return self.dma(q, lambda h: h.dma_start(out=out, in_=in_, allow_slow_non_contiguous=True), slot, r, w)
        return self.dma(q, lambda h: h.dma_start(out=out, in_=in_), slot, r, w)

D = 1024
E = 2048
TPR = 2048
TSG = 4096
TOWN = TPR + TSG
TG = 4 * TOWN
NVEC = 96
RT = 256
CH = 128
QK_SCALE = 192.0 ** -0.5
NEG_E05 = -0.6065306597126334


def build(nlayers=4, DBG=9, SUB=9):
    import os
    K5 = int(os.environ.get('K5', '9'))
    K7 = int(os.environ.get('K7', '9'))
    K8 = int(os.environ.get('K8', '9'))
    NSEQ = int(os.environ.get('NSEQ', '99'))
    NTL = int(os.environ.get('NTL', '9999'))
    nc = bass.Bass("TRN2", target_bir_lowering=False)
    P = Prog(nc)

    def din(name, shape, dt=F32):
        return nc.dram_tensor(name, list(shape), dt, kind="ExternalInput")

    xin = din("xin", [TOWN, D])
    consts_d = din("consts", [128, 768])
    rope_d = din("rope", [TOWN, 64])
    lng_d = din("lng", [5, 128, D])
    RW = []
    ML = []
    for j in range(2):
        RW.append(dict(wloc=din(f"rw_wloc{j}", [D, 2304]), w2=din(f"rw_w2{j}", [128, 512]),
                       a2=din(f"rw_a2{j}", [128, 512]), wout=din(f"rw_wout{j}", [512, D]),
                       vec=din(f"rw_vec{j}", [128, NVEC])))
        ML.append(dict(inA=din(f"ml_inA{j}", [D, 704]), inG=din(f"ml_inG{j}", [D, E]),
                       uqn=din(f"ml_uqn{j}", [384, 2048]), uqr=din(f"ml_uqr{j}", [384, 1024]),
                       ukvk=din(f"ml_ukvk{j}", [256, 2048]), ukvv=din(f"ml_ukvv{j}", [256, 2048]),
                       out=din(f"ml_out{j}", [E, D]), qn=din(f"ml_qn{j}", [128, 384]),
                       kvn=din(f"ml_kvn{j}", [128, 256])))
    yout = nc.dram_tensor("y", [TOWN, D], F32, kind="ExternalOutput")

    x_own = nc.dram_tensor("x_own", [TOWN, D], F32)
    NCK = TOWN // 512
    hT_loc = [nc.dram_tensor(f"hT_loc{k}", [D, 512], BF16) for k in range(NCK)]
    hT_all = [nc.dram_tensor(f"hT_all{k}", [4 * D, 512], BF16) for k in range(NCK)]
    yf_d = nc.dram_tensor("yf_d", [512, TG], F32)
    part_d = [nc.dram_tensor(f"part_d{k}", [4 * 512, D], F32) for k in range(NCK)]
    delta_d = [nc.dram_tensor(f"delta_d{k}", [512, D], F32) for k in range(NCK)]
    lat_loc = [nc.dram_tensor(f"lat_loc{k}", [384, 512], BF16) for k in range(NCK)]
    lat_all = [nc.dram_tensor(f"lat_all{k}", [4 * 384, 512], BF16) for k in range(NCK)]
    cqT_d = nc.dram_tensor("cqT_d", [384, TOWN], BF16)
    qpeT_d = nc.dram_tensor("qpeT_d", [1024, TOWN], BF16)
    o_scr = nc.dram_tensor("o_scr", [TOWN, E], BF16)
    groups = [[0, 1, 2, 3], [4, 5, 6, 7]]

    _uid = [0]

    def uname(n):
        _uid[0] += 1
        return f"{n}_u{_uid[0]}"

    cst = nc.alloc_sbuf_tensor("cst", [128, 768], F32)
    idb = nc.alloc_sbuf_tensor("idb", [128, 128], BF16)
    def alloc_psum(es, nf, nb):
        pf = [es.enter_context(nc.psum_tensor(uname(f"psF{i}"), [128, 512], F32)) for i in range(nf)]
        pb = [es.enter_context(nc.psum_tensor(uname(f"psB{i}"), [128, 1024], BF16)) for i in range(nb)]
        return pf, pb
    P.load(cst[:], consts_d.ap(), "cst", w=["cst"])
    P.cp('dve', idb[:], cst[:, 0:128], ["cst"], ["idb"])
    IDENT = idb[:]
    MF = cst[:, 128:384]
    MB = cst[:, 384:640]
    BONES = cst[:, 640:768]


    def load_w_bf16(es, name, src, nchunk, ncols, rows=128):
        dst = es.enter_context(nc.sbuf_tensor(name, [128, nchunk, ncols], BF16))
        return dst

    def fill_w_bf16(dst, name, src, nchunk, ncols, stg, rows=128):
        for c in range(nchunk):
            for c0 in range(0, ncols, 1024):
                cw = min(1024, ncols - c0)
                b = (c + c0 // 1024) % 2
                P.load(stg[b][:rows, :cw], src[c * rows:(c + 1) * rows, c0:c0 + cw], f"stg{b}", w=[f"stg{b}"])
                eng = 'pool' if b == 0 else 'act'
                P.cp(eng, dst[:rows, c, c0:c0 + cw], stg[b][:rows, :cw], [f"stg{b}"], [name])

    def norm_phase(xsrc, delta, xdst, gidx, final=False):
        P.barrier()
        with ExitStack() as es:
            sb = lambda n, s, d=F32: es.enter_context(nc.sbuf_tensor(uname(n), s, d))
            psF, psB = alloc_psum(es, 0, 2)
            gt = sb("np_g", [128, D])
            xt = [sb(f"np_x{i}", [128, D]) for i in range(2)]
            dl = [sb(f"np_d{i}", [128, D]) for i in range(2)]
            sq = sb("np_sq", [128, D])
            st = [sb(f"np_s{i}", [128, 4]) for i in range(2)]
            hb = [sb(f"np_h{i}", [128, D], BF16) for i in range(2)]
            hf = [sb(f"np_hf{i}", [128, D]) for i in range(2)]
            hTt = [sb(f"np_hT{i}", [128, 8, 512], BF16) for i in range(2)]
            P.load(gt[:], lng_d.ap()[gidx], "np_g", w=["np_g"])
            nt = TOWN // 128
            for t in range(nt):
                b = t % 2
                rows = slice(t * 128, (t + 1) * 128)
                P.load(xt[b][:], xsrc.ap()[rows, :], f"np_x{b}", w=[f"np_x{b}"])
                if delta is not None:
                    P.load(dl[b][:], delta[t // 4].ap()[(t % 4) * 128:(t % 4 + 1) * 128, :], f"np_d{b}", w=[f"np_d{b}"])
                    P.tt('pool', xt[b][:], xt[b][:], dl[b][:], ALU.add, [f"np_x{b}", f"np_d{b}"], [f"np_x{b}"])
                    P.load(xdst.ap()[rows, :], xt[b][:], f"np_xs{b}", r=[f"np_x{b}"], w=[], q='pool')
                P.memset('dve', st[b][:], 0.0, w=[f"np_s{b}"])
                P.act(sq[:], xt[b][:], AF.Square, accum=st[b][:, 0:1], r=[f"np_x{b}", f"np_s{b}"], w=["np_sq", f"np_s{b}"])
                P.ts('dve', st[b][:, 1:2], st[b][:, 0:1], 1.0 / D, 1e-6, ALU.mult, ALU.add, r=[f"np_s{b}"], w=[f"np_s{b}"])
                P.act(st[b][:, 2:3], st[b][:, 1:2], AF.Sqrt, r=[f"np_s{b}"], w=[f"np_s{b}"])
                P.op('dve', lambda h, b=b: h.reciprocal(st[b][:, 3:4], st[b][:, 2:3]), [f"np_s{b}"], [f"np_s{b}"])
                if final:
                    P.stt('dve', hf[b][:], xt[b][:], st[b][:, 3:4], gt[:], ALU.mult, ALU.mult,
                          r=[f"np_x{b}", f"np_s{b}", "np_g"], w=[f"np_hf{b}"])
                    P.load(yout.ap()[rows, :], hf[b][:], f"np_ys{b}", r=[f"np_hf{b}"], w=[], q='pool')
                    continue
                P.stt('dve', hb[b][:], xt[b][:], st[b][:, 3:4], gt[:], ALU.mult, ALU.mult,
                      r=[f"np_x{b}", f"np_s{b}", "np_g"], w=[f"np_h{b}"])
                pt = psB[b]
                for c in range(8):
                    P.tr(pt[:, c * 128:(c + 1) * 128], hb[b][:, c * 128:(c + 1) * 128], IDENT, [f"np_h{b}", "idb"], [f"psB{b}"])
                g4 = (t // 4) % 2
                sub = t % 4
                P.cp('act', hTt[g4][:, :, sub * 128:(sub + 1) * 128], pt[:].rearrange("p (c t) -> p c t", c=8),
                     [f"psB{b}"], [f"np_hT{g4}"])
                if sub == 3:
                    P.load(hT_loc[t // 4].ap().rearrange("(c p) t -> p c t", p=128), hTt[g4][:],
                           f"np_hTs{g4}", r=[f"np_hT{g4}"], w=[], q='pool')
        P.barrier()

    def collective(kind, op, srcs, dsts, name):
        P.barrier()
        for k, (src, dst) in enumerate(zip(srcs, dsts)):
            P.dma('pool', lambda h, src=src, dst=dst: h.collective_compute(kind, op, replica_groups=groups,
                                                                         ins=[src.ap().opt()], outs=[dst.ap().opt()]),
                  f"cc_{name}_{k % 4}", [], [], inc=1)
        P.barrier()

    def seq_list():
        seqs = [[(rb, 0, TPR)] for rb in range(4)]
        seqs.append([(rb, TPR, TSG) for rb in range(4)])
        return seqs

    def rwkv_layer(j):
        W = RW[j]
        P.barrier()
        with ExitStack() as es:
            sb = lambda n, s, d=F32: es.enter_context(nc.sbuf_tensor(uname(n), s, d))
            psF, psB = alloc_psum(es, 7, 1)
            wbf = sb("rw_wbf", [128, 8, 2304], BF16)
            w2bf = sb("rw_w2bf", [128, 1, 512], BF16)
            a2bf = sb("rw_a2bf", [128, 1, 512], BF16)
            wobf = sb("rw_wobf", [128, 4, D], BF16)
            vec = sb("rw_vec", [128, NVEC])
            stg = [sb(f"stg{i}", [128, 1024]) for i in range(2)]
            fill_w_bf16(wbf, "rw_wbf", W["wloc"].ap(), 8, 2304, stg)
            fill_w_bf16(w2bf, "rw_w2bf", W["w2"].ap(), 1, 512, stg)
            fill_w_bf16(a2bf, "rw_a2bf", W["a2"].ap(), 1, 512, stg)
            fill_w_bf16(wobf, "rw_wobf", W["wout"].ap(), 4, D, stg)
            P.load(vec[:], W["vec"].ap(), "rw_vec", w=["rw_vec"])
            P.ts('dve', vec[:, 84:88], vec[:, 68:72], -1.0, 1.0, ALU.mult, ALU.add, r=["rw_vec"], w=["rw_vec"])
            P.ts('dve', vec[:, 88:92], vec[:, 68:72], -2.0, 2.0, ALU.mult, ALU.add, r=["rw_vec"], w=["rw_vec"])
            VW0 = lambda d, hp: vec[:, 48 + 4 * d + hp:49 + 4 * d + hp]
            VA0 = lambda d, hp: vec[:, 56 + 4 * d + hp:57 + 4 * d + hp]
            VKK = lambda hp: vec[:, 64 + hp:65 + hp]
            VKA = lambda hp: vec[:, 68 + hp:69 + hp]
            VRK = lambda hp: vec[:, 72 + hp:73 + hp]
            VLG = lambda hp: vec[:, 76 + hp:77 + hp]
            VLB = lambda hp: vec[:, 80 + hp:81 + hp]
            VOM = lambda hp: vec[:, 84 + hp:85 + hp]
            VOM2 = lambda hp: vec[:, 88 + hp:89 + hp]
            VMU = lambda c, s: vec[:, c * 6 + s:c * 6 + s + 1]

            hx = [sb(f"rw_hx{i}", [128, 8, RT + 2], BF16) for i in range(2)]
            xx = sb("rw_xx", [128, 8, RT])
            lerp = [sb(f"rw_lp{i}", [128, 8, RT], BF16) for i in range(2)]
            pr = {s: [sb(f"rw_p{s}{hp}", [128, RT]) for hp in range(4)] for s in "rkvg"}
            wlt = [sb(f"rw_wl{d}", [128, RT], BF16) for d in range(2)]
            alt = [sb(f"rw_al{d}", [128, RT], BF16) for d in range(2)]
            kk = sb("rw_kk", [128, RT]); kk2 = sb("rw_kk2", [128, RT]); kkn = sb("rw_kkn", [128, RT])
            lw = sb("rw_lw", [128, RT]); av = [sb(f"rw_a{d}", [128, RT]) for d in range(2)]
            tmp = sb("rw_tmp", [128, RT]); kd = sb("rw_kd", [128, RT]); bb = sb("rw_b", [128, RT])
            cum = sb("rw_cum", [128, RT]); ex = sb("rw_ex", [128, RT]); ones = sb("rw_ones", [128, RT])
            krt = sb("rw_krt", [128, 2, 2, CH], BF16)
            bhb = sb("rw_bhb", [128, RT], BF16)
            khbZ = [sb(f"rw_khbZ{i}", [128, RT], BF16) for i in range(2)]
            bhbZ = [sb(f"rw_bhbZ{i}", [128, RT], BF16) for i in range(2)]
            kktZ = [sb(f"rw_kktZ{i}", [128, 2, CH], BF16) for i in range(2)]
            Kbb = sb("rw_Kbb", [128, RT], BF16); Bbb = sb("rw_Bbb", [128, RT], BF16)
            vbf = sb("rw_vbf", [128, RT], BF16)
            gC = sb("rw_gC", [128, 2])
            tokm = [sb(f"rw_tokm{i}", [128, 4, CH], BF16) for i in range(2)]
            vpad = [sb(f"rw_vpad{i}", [128, 2, CH], BF16) for i in range(2)]
            upad = [sb(f"rw_upad{i}", [128, 2, CH], BF16) for i in range(2)]
            ucat = [sb(f"rw_ucat{i}", [128, CH], BF16) for i in range(2)]
            AbT = [sb(f"rw_AbT{i}", [128, 2 * CH], BF16) for i in range(2)]
            AkT = [sb(f"rw_AkT{i}", [128, 2 * CH], BF16) for i in range(2)]
            Gp = [sb(f"rw_G{i}", [128, 2, CH], BF16) for i in range(2)]
            TTp = [sb(f"rw_TT{i}", [128, CH], BF16) for i in range(2)]
            TTh = [sb(f"rw_TTh{i}", [128, CH], BF16) for i in range(2)]
            WTb = sb("rw_WT", [128, CH], BF16)
            AVb = sb("rw_AV", [128, 2, 64], BF16)
            H32 = [sb(f"rw_H32{hp}", [128, CH]) for hp in range(4)]
            Hbf = [sb(f"rw_Hbf{hp}", [128, CH], BF16) for hp in range(4)]
            htmp = sb("rw_htmp", [128, CH])
            yT = [sb(f"rw_yT{hp}", [128, RT]) for hp in range(4)]
            yfl = [sb(f"rw_yfl{hp}", [128, RT]) for hp in range(4)]
            zb = sb("rw_zb", [128, 4, RT], BF16)
            po = [sb(f"rw_po{i}", [128, D]) for i in range(2)]
            q1 = sb("rw_q1", [128, RT]); q2 = sb("rw_q2", [128, RT]); q3 = sb("rw_q3", [128, RT])

            P.memset('pool', ones[:], 1.0, w=["rw_ones"])
            P.memset('pool', WTb[:], 0.0, w=["rw_WT"])
            for i in range(2):
                P.memset('pool', khbZ[i][:], 0.0, w=[f"rw_khbZ{i}"])
                P.memset('pool', bhbZ[i][:], 0.0, w=[f"rw_bhbZ{i}"])
                P.memset('pool', kktZ[i][:], 0.0, w=[f"rw_kktZ{i}"])
                P.memset('pool', wlt[i][:], 0.0, w=[f"rw_wl{i}"])
                P.memset('pool', alt[i][:], 0.0, w=[f"rw_al{i}"])
            for i in range(2):
                P.memset('pool', vpad[i][:], 0.0, w=[f"rw_vpad{i}"])
                P.memset('pool', upad[i][:], 0.0, w=[f"rw_upad{i}"])

            def diag_ap(t):
                base = t[:]
                return bass.AP(base.tensor, base.offset, [list(base.ap[0]), [CH + 64, 2], [1, 64]])

            chunk_ctr = [0]

            def sweep(d):
                MA = MF if d == 0 else MB
                MP = MB[:, 0:128] if d == 0 else MF[:, 0:128]
                for seq in seq_list()[:NSEQ]:
                    tiles = []
                    for pi, (rb, c0, n) in enumerate(seq):
                        for t0 in range(0, n, RT):
                            tiles.append((pi, rb, c0 + t0))
                    ntl = len(tiles)
                    order = list(range(ntl) if d == 0 else range(ntl - 1, -1, -1))[:NTL]
                    for hp in range(4):
                        P.memset('pool', H32[hp][:], 0.0, w=[f"rw_H32{hp}"])
                        P.memset('pool', Hbf[hp][:], 0.0, w=[f"rw_Hbf{hp}"])
                    for ti in order:
                        pi, rb, bc = tiles[ti]
                        gcol = rb * TOWN + bc
                        hb_ = ti % 2
                        hxt = hx[hb_]
                        hk = f"rw_hx{hb_}"
                        def hcol(rb_, col, n):
                            return hT_all[col // 512].ap()[rb_ * D:(rb_ + 1) * D, col % 512:col % 512 + n].rearrange("(c p) t -> p c t", p=128)
                        P.load(hxt[:, :, 1:RT + 1], hcol(rb, bc, RT), hk, w=[hk])
                        if ti == 0:
                            P.memset('pool', hxt[:, :, 0:1], 0.0, w=[hk])
                        else:
                            _, prb, pbc = tiles[ti - 1]
                            P.load(hxt[:, :, 0:1], hcol(prb, pbc + RT - 1, 1), hk + "h", w=[hk], slow=True)
                        if ti == ntl - 1:
                            P.memset('pool', hxt[:, :, RT + 1:RT + 2], 0.0, w=[hk])
                        else:
                            _, nrb, nbc = tiles[ti + 1]
                            P.load(hxt[:, :, RT + 1:RT + 2], hcol(nrb, nbc, 1), hk + "g", w=[hk], slow=True)
                        if d == 1:
                            for hp in range(4):
                                P.load(yfl[hp][:], yf_d.ap()[hp * 128:(hp + 1) * 128, gcol:gcol + RT], f"rw_yfl{hp}", w=[f"rw_yfl{hp}"])
                        P.tt('pool', xx[:], hxt[:, :, 0:RT], hxt[:, :, 2:RT + 2], ALU.add, [hk], ["rw_xx"])
                        P.stt('dve', xx[:], xx[:], 0.5, hxt[:, :, 1:RT + 1], ALU.mult, ALU.subtract, [hk, "rw_xx"], ["rw_xx"])
                        if SUB < 2:
                            continue
                        streams = [(0, 0, 'r'), (2, 512, 'k'), (3, 1024, 'v')]
                        if d == 1:
                            streams.append((5, 1536, 'g'))
                        streams += [(1, 2048, 'w'), (4, 2176, 'a')]
                        pcount = 0
                        for si, (ls, coff, nm) in enumerate(streams):
                            lb = si % 2
                            lk = f"rw_lp{lb}"
                            for c in range(8):
                                P.stt('dve', lerp[lb][:, c, :], xx[:, c, :], VMU(c, ls), hxt[:, c, 1:RT + 1],
                                      ALU.mult, ALU.add, [hk, "rw_xx", "rw_vec"], [lk])
                            if nm in "rkvg":
                                for hp in range(4):
                                    pb = psF[pcount % 2]; pk = f"psF{pcount % 2}"; pcount += 1
                                    for c in range(8):
                                        P.mm(pb[:, 0:RT], wbf[:, c, coff + hp * 128:coff + (hp + 1) * 128], lerp[lb][:, c, :],
                                             start=(c == 0), stop=(c == 7), r=["rw_wbf", lk], w=[pk])
                                    P.cp('act', pr[nm][hp][:], pb[:, 0:RT], [pk], [f"rw_p{nm}{hp}"])
                            else:
                                dirs = [d] if (nm == 'w' or d == 0) else [0, 1]
                                pb = psF[pcount % 2]; pk = f"psF{pcount % 2}"; pcount += 1
                                for c in range(8):
                                    P.mm(pb[:, 0:RT], wbf[:, c, coff:coff + 128], lerp[lb][:, c, :],
                                         start=(c == 0), stop=(c == 7), r=["rw_wbf", lk], w=[pk])
                                for dd in dirs:
                                    lo = dd * 64
                                    if nm == 'w':
                                        P.act(wlt[dd][lo:lo + 64, :], pb[lo:lo + 64, 0:RT], AF.Tanh, r=[pk], w=[f"rw_wl{dd}"])
                                    else:
                                        P.cp('act', alt[dd][lo:lo + 64, :], pb[lo:lo + 64, 0:RT], [pk], [f"rw_al{dd}"])
                        for hp in range(4 if SUB >= 3 else 0):
                            rk_, kk_, vk_, gk_ = (f"rw_p{s}{hp}" for s in "rkvg")
                            r_t, k_t, v_t, g_t = (pr[s][hp] for s in "rkvg")
                            csl = slice(hp * 128, (hp + 1) * 128)
                            ps2 = psF[2]
                            P.ts('dve', kk[:], k_t[:], VKK(hp), None, ALU.mult, r=[kk_, "rw_vec"], w=["rw_kk"])
                            P.tt('dve', kk2[:], kk[:], kk[:], ALU.mult, ["rw_kk"], ["rw_kk2"])
                            P.mm(ps2[:, 0:RT], BONES, kk2[:], r=["cst", "rw_kk2"], w=["psF2"])
                            P.act(kk2[:], ps2[:, 0:RT], AF.Sqrt, r=["psF2"], w=["rw_kk2"])
                            P.ts('dve', kk2[:], kk2[:], 1e-12, None, ALU.max, r=["rw_kk2"], w=["rw_kk2"])
                            P.op('dve', lambda h: h.reciprocal(kk2[:], kk2[:]), ["rw_kk2"], ["rw_kk2"])
                            P.tt('dve', kkn[:], kk[:], kk2[:], ALU.mult, ["rw_kk", "rw_kk2"], ["rw_kkn"])
                            lo = d * 64
                            P.mm(ps2[:, RT:2 * RT], w2bf[:, 0, csl], wlt[d][:, :], r=["rw_w2bf", f"rw_wl{d}"], w=["psF2"])
                            P.act(lw[:], ps2[:, RT:2 * RT], AF.Sigmoid, bias=VW0(d, hp), r=["psF2", "rw_vec"], w=["rw_lw"])
                            P.ts('pool', lw[:], lw[:], NEG_E05, None, ALU.mult, r=["rw_lw"], w=["rw_lw"])
                            adirs = [d] if d == 0 else [1, 0]
                            for dd in adirs:
                                lo2 = dd * 64
                                P.mm(ps2[:, 0:RT], a2bf[:, 0, csl], alt[dd][:, :], r=["rw_a2bf", f"rw_al{dd}"], w=["psF2"])
                                P.act(av[dd][:], ps2[:, 0:RT], AF.Sigmoid, bias=VA0(dd, hp), r=["psF2", "rw_vec"], w=[f"rw_a{dd}"])
                            P.ts('dve', tmp[:], av[d][:], VKA(hp), VOM(hp), ALU.mult, ALU.add, r=[f"rw_a{d}", "rw_vec"], w=["rw_tmp"])
                            P.tt('dve', kd[:], k_t[:], tmp[:], ALU.mult, [kk_, "rw_tmp"], ["rw_kd"])
                            P.tt('pool', bb[:], kkn[:], av[d][:], ALU.mult, ["rw_kkn", f"rw_a{d}"], ["rw_b"])
                            if d == 1:
                                P.tt('pool', q1[:], av[0][:], av[1][:], ALU.add, ["rw_a0", "rw_a1"], ["rw_q1"])
                                P.ts('pool', q1[:], q1[:], VKA(hp), VOM2(hp), ALU.mult, ALU.add, r=["rw_q1", "rw_vec"], w=["rw_q1"])
                                P.tt('pool', q1[:], q1[:], k_t[:], ALU.mult, ["rw_q1", kk_], ["rw_q1"])
                                P.stt('dve', q1[:], r_t[:], VRK(hp), q1[:], ALU.mult, ALU.mult, [rk_, "rw_vec", "rw_q1"], ["rw_q1"])
                                P.mm(ps2[:, RT:2 * RT], BONES, q1[:], r=["cst", "rw_q1"], w=["psF2"])
                                P.tt('dve', q3[:], ps2[:, RT:2 * RT], v_t[:], ALU.mult, ["psF2", vk_], ["rw_q3"])
                            for ci in range(2):
                                cs = slice(ci * CH, (ci + 1) * CH)
                                P.op('dve', lambda h, cs=cs: h.tensor_tensor_scan(cum[:, cs], ones[:, cs], lw[:, cs], 0.0, ALU.mult, ALU.add),
                                     ["rw_ones", "rw_lw"], ["rw_cum"])
                                if d == 0:
                                    P.cp('pool', gC[:, ci:ci + 1], cum[:, ci * CH + CH - 1:ci * CH + CH], ["rw_cum"], ["rw_gC"])
                                else:
                                    P.cp('pool', gC[:, ci:ci + 1], cum[:, ci * CH + CH - 1:ci * CH + CH], ["rw_cum"], ["rw_gC"])
                                    P.tt('dve', cum[:, cs], lw[:, cs], cum[:, cs], ALU.subtract, ["rw_lw", "rw_cum"], ["rw_cum"])
                                    P.ts('dve', cum[:, cs], cum[:, cs], gC[:, ci:ci + 1], None, ALU.add, r=["rw_cum", "rw_gC"], w=["rw_cum"])
                            krv = krt[:].rearrange("p c j t -> p c (j t)")
                            P.act(ex[:], cum[:], AF.Exp, r=["rw_cum"], w=["rw_ex"])
                            P.tt('dve', krt[:, :, 1, :], r_t[:].rearrange("p (c t) -> p c t", c=2), ex[:].rearrange("p (c t) -> p c t", c=2),
                                 ALU.mult, [rk_, "rw_ex"], ["rw_krt"])
                            P.tt('pool', tmp[:], cum[:], lw[:], ALU.subtract, ["rw_cum", "rw_lw"], ["rw_tmp"])
                            P.act(ex[:], tmp[:], AF.Exp, r=["rw_tmp"], w=["rw_ex"])
                            P.tt('dve', krt[:, :, 0, :], kkn[:].rearrange("p (c t) -> p c t", c=2), ex[:].rearrange("p (c t) -> p c t", c=2),
                                 ALU.mult, ["rw_kkn", "rw_ex"], ["rw_krt"])
                            for hz in range(2):
                                pz = slice(hz * 64, hz * 64 + 64)
                                P.tt('pool', kktZ[hz][pz, :, :], kkn[pz, :].rearrange("p (c t) -> p c t", c=2), ex[pz, :].rearrange("p (c t) -> p c t", c=2),
                                     ALU.mult, ["rw_kkn", "rw_ex"], [f"rw_kktZ{hz}"])
                            P.act(ex[:], cum[:], AF.Exp, scale=-1.0, r=["rw_cum"], w=["rw_ex"])
                            P.tt('pool', bhb[:], bb[:], ex[:], ALU.mult, ["rw_b", "rw_ex"], ["rw_bhb"])
                            for hz in range(2):
                                pz = slice(hz * 64, hz * 64 + 64)
                                P.tt('dve', khbZ[hz][pz, :], kd[pz, :], ex[pz, :], ALU.mult, ["rw_kd", "rw_ex"], [f"rw_khbZ{hz}"])
                                P.tt('pool', bhbZ[hz][pz, :], bb[pz, :], ex[pz, :], ALU.mult, ["rw_b", "rw_ex"], [f"rw_bhbZ{hz}"])
                            for ci in range(2):
                                cs = slice(ci * CH, (ci + 1) * CH)
                                P.act(ex[:, cs], cum[:, cs], AF.Exp, bias=gC[:, ci:ci + 1], scale=-1.0, r=["rw_cum", "rw_gC"], w=["rw_ex"])
                            P.tt('dve', Kbb[:], kd[:], ex[:], ALU.mult, ["rw_kd", "rw_ex"], ["rw_Kbb"])
                            P.tt('pool', Bbb[:], bb[:], ex[:], ALU.mult, ["rw_b", "rw_ex"], ["rw_Bbb"])
                            P.act(gC[:], gC[:], AF.Exp, r=["rw_gC"], w=["rw_gC"])
                            P.cp('pool', vbf[:], v_t[:], [vk_], ["rw_vbf"])
                            for ci in (([0, 1] if d == 0 else [1, 0]) if SUB >= 4 else []):
                                cs = slice(ci * CH, (ci + 1) * CH)
                                cb = chunk_ctr[0] % 2
                                chunk_ctr[0] += 1
                                tk = f"rw_tokm{cb}"
                                ptk = psB[0]
                                P.tr(ptk[:, 0:128], krt[:, ci, 0, :], IDENT, ["rw_krt", "idb"], ["psB0"])
                                P.tr(ptk[:, 128:256], Kbb[:, cs], IDENT, ["rw_Kbb", "idb"], ["psB0"])
                                P.tr(ptk[:, 256:384], Bbb[:, cs], IDENT, ["rw_Bbb", "idb"], ["psB0"])
                                P.tr(ptk[:, 384:512], vbf[:, cs], IDENT, ["rw_vbf", "idb"], ["psB0"])
                                P.cp('act', tokm[cb][:], ptk[:, 0:512].rearrange("p (a t) -> p a t", a=4), ["psB0"], [tk])
                                P.cp('pool', diag_ap(vpad[cb]), tokm[cb][:, 3, :].rearrange("p (h j) -> p h j", h=2), [tk], [f"rw_vpad{cb}"])
                                if SUB < 5:
                                    continue
                                for hh in range(2):
                                    pl = slice(hh * 64, hh * 64 + 64)
                                    pA = psF[3]
                                    P.mm(pA[:, 0:256], bhbZ[hh][:, cs], krv[:, ci, :], r=[f"rw_bhbZ{hh}", "rw_krt"], w=["psF3"])
                                    P.mm(pA[:, 256:512], khbZ[hh][:, cs], krv[:, ci, :], r=[f"rw_khbZ{hh}", "rw_krt"], w=["psF3"])
                                    p4 = psF[4]
                                    P.mm(p4[:, 0:128], kktZ[hh][:, ci, :], bhb[:, cs], r=[f"rw_kktZ{hh}", "rw_bhb"], w=["psF4"])
                                    if K5 < 2:
                                        continue
                                    P.tt('dve', AbT[hh][:], pA[:, 0:256], MA, ALU.mult, ["psF3", "cst"], [f"rw_AbT{hh}"])
                                    P.tt('dve', AkT[hh][:], pA[:, 256:512], MA, ALU.mult, ["psF3", "cst"], [f"rw_AkT{hh}"])
                                    if K5 < 3:
                                        continue
                                    P.stt('dve', Gp[0][:, 0, :], p4[:, 0:128], -1.0, MP, ALU.mult, ALU.mult, ["psF4", "cst"], ["rw_G0"])
                                    if K5 < 4:
                                        continue
                                    P.ts('pool', Gp[0][:, 1, :], AbT[hh][:, 0:128], -1.0, None, ALU.mult, r=[f"rw_AbT{hh}"], w=["rw_G0"])
                                    P.tt('pool', TTp[0][:], IDENT, AbT[hh][:, 0:128], ALU.subtract, ["idb", f"rw_AbT{hh}"], ["rw_TT0"])
                                    gi = 0
                                    ti_ = 0
                                    if SUB < 6:
                                        continue
                                    for lvl in range(1, 7):
                                        Gc = Gp[gi]; Gn = Gp[1 - gi]
                                        gck = f"rw_G{gi}"; gnk = f"rw_G{1 - gi}"
                                        P.mm(p4[:, 128:256], Gc[:, 1, :], Gc[:, 0, :], r=[gck], w=["psF4"])
                                        if lvl < 6:
                                            P.mm(p4[:, 256:384], Gc[:, 0, :], Gc[:, 1, :], r=[gck], w=["psF4"])
                                            P.cp('act', Gn[:], p4[:, 128:384].rearrange("p (a t) -> p a t", a=2), ["psF4"], [gnk])
                                        else:
                                            P.cp('act', Gn[:, 0, :], p4[:, 128:256], ["psF4"], [gnk])
                                        Tc = TTp[ti_]; tck = f"rw_TT{ti_}"
                                        if lvl < 6:
                                            Tn = TTp[1 - ti_]; tnk = f"rw_TT{1 - ti_}"
                                        else:
                                            Tn = TTh[hh]; tnk = f"rw_TTh{hh}"
                                        P.mm(psF[5][:, 0:128], Gn[:, 0, :], Tc[:], r=[gnk, tck], w=["psF5"])
                                        P.tt('dve', Tn[:], psF[5][:, 0:128], Tc[:], ALU.add, ["psF5", tck], [tnk])
                                        gi = 1 - gi
                                        ti_ = 1 - ti_
                                    if SUB < 7:
                                        continue
                                    p5 = psF[5]
                                    P.mm(p5[:, 128:256], tokm[cb][:, 0, :], TTh[hh][:], r=[tk, f"rw_TTh{hh}"], w=["psF5"])
                                    if K7 < 2:
                                        continue
                                    P.cp('act', WTb[pl, :], p5[pl, 128:256], ["psF5"], ["rw_WT"])
                                    if K7 < 3:
                                        continue
                                    P.mm(p5[:, 256 + hh * 64:320 + hh * 64], AkT[hh][:, 0:128], tokm[cb][:, 3, pl], r=[f"rw_AkT{hh}", tk], w=["psF5"])
                                    if K7 < 4:
                                        continue
                                    P.cp('act', AVb[:, hh, :], p5[:, 256 + hh * 64:320 + hh * 64], ["psF5"], ["rw_AV"])
                                if SUB < 8:
                                    continue
                                p5 = psF[6]
                                hk2 = f"rw_Hbf{hp}"
                                for hh in range(2):
                                    pl = slice(hh * 64, hh * 64 + 64)
                                    P.mm(p5[:, 0 + hh * 64:64 + hh * 64], TTh[hh][:], AVb[:, hh, :], start=True, stop=False,
                                         r=[f"rw_TTh{hh}", "rw_AV"], w=["psF6"])
                                    P.mm(p5[:, 0 + hh * 64:64 + hh * 64], WTb[:, :], Hbf[hp][:, pl], start=False, stop=True,
                                         r=["rw_WT", hk2], w=["psF6"])
                                if K8 < 2:
                                    continue
                                P.ts('dve', ucat[cb][:], p5[:, 0:128], -1.0, None, ALU.mult, r=["psF6"], w=[f"rw_ucat{cb}"])
                                P.cp('pool', diag_ap(upad[cb]), ucat[cb][:].rearrange("p (h j) -> p h j", h=2), [f"rw_ucat{cb}"], [f"rw_upad{cb}"])
                                if K8 < 3:
                                    continue
                                pY = p5[:, 128:256]
                                P.mm(pY, Hbf[hp][:], krt[:, ci, 1, :], start=True, stop=False, r=[hk2, "rw_krt"], w=["psF6"])
                                for hh in range(2):
                                    P.mm(pY, vpad[cb][:, hh, :], AkT[hh][:, 128:256], start=False, stop=False,
                                         r=[f"rw_vpad{cb}", f"rw_AkT{hh}"], w=["psF6"])
                                    P.mm(pY, upad[cb][:, hh, :], AbT[hh][:, 128:256], start=False, stop=(hh == 1),
                                         r=[f"rw_upad{cb}", f"rw_AbT{hh}"], w=["psF6"])
                                if K8 < 4:
                                    continue
                                if d == 0:
                                    P.cp('act', yT[hp][:, cs], pY, ["psF6"], [f"rw_yT{hp}"])
                                else:
                                    P.tt('dve', yT[hp][:, cs], pY, yfl[hp][:, cs], ALU.add, ["psF6", f"rw_yfl{hp}"], [f"rw_yT{hp}"])
                                if K8 < 5:
                                    continue
                                pH = p5[:, 256:384]
                                P.mm(pH, tokm[cb][:, 1, :], tokm[cb][:, 3, :], start=True, stop=False, r=[tk], w=["psF6"])
                                P.mm(pH, tokm[cb][:, 2, :], ucat[cb][:], start=False, stop=True, r=[tk, f"rw_ucat{cb}"], w=["psF6"])
                                if K8 < 6:
                                    continue
                                P.tt('dve', htmp[:], pH, BONES, ALU.mult, ["psF6", "cst"], ["rw_htmp"])
                                P.stt('dve', H32[hp][:], H32[hp][:], gC[:, ci:ci + 1], htmp[:], ALU.mult, ALU.add,
                                      [f"rw_H32{hp}", "rw_gC", "rw_htmp"], [f"rw_H32{hp}"])
                                P.cp('act', Hbf[hp][:], H32[hp][:], [f"rw_H32{hp}"], [hk2])
                            if d == 0:
                                P.load(yf_d.ap()[hp * 128:(hp + 1) * 128, gcol:gcol + RT], yT[hp][:], f"rw_yfs{hp}",
                                       r=[f"rw_yT{hp}"], w=[], q='pool')
                            else:
                                ps2 = psF[2]
                                y_ = yT[hp]
                                P.mm(ps2[:, 0:RT], BONES, y_[:], r=["cst", f"rw_yT{hp}"], w=["psF2"])
                                P.stt('dve', q1[:], ps2[:, 0:RT], -1.0 / 64, y_[:], ALU.mult, ALU.add, ["psF2", f"rw_yT{hp}"], ["rw_q1"])
                                P.tt('dve', q2[:], q1[:], q1[:], ALU.mult, ["rw_q1"], ["rw_q2"])
                                P.mm(ps2[:, RT:2 * RT], BONES, q2[:], r=["cst", "rw_q2"], w=["psF2"])
                                P.ts('dve', q2[:], ps2[:, RT:2 * RT], 1.0 / 64, 64e-5, ALU.mult, ALU.add, r=["psF2"], w=["rw_q2"])
                                P.act(q2[:], q2[:], AF.Sqrt, r=["rw_q2"], w=["rw_q2"])
                                P.op('dve', lambda h: h.reciprocal(q2[:], q2[:]), ["rw_q2"], ["rw_q2"])
                                P.tt('dve', q1[:], q1[:], q2[:], ALU.mult, ["rw_q1", "rw_q2"], ["rw_q1"])
                                P.ts('dve', q1[:], q1[:], VLG(hp), VLB(hp), ALU.mult, ALU.add, r=["rw_q1", "rw_vec"], w=["rw_q1"])
                                P.tt('dve', q1[:], q1[:], q3[:], ALU.add, ["rw_q1", "rw_q3"], ["rw_q1"])
                                P.act(q2[:], g_t[:], AF.Silu, r=[gk_], w=["rw_q2"])
                                P.tt('dve', zb[:, hp, :], q1[:], q2[:], ALU.mult, ["rw_q1", "rw_q2"], ["rw_zb"])
                        if d == 1:
                            for s2 in range(RT // 128):
                                pob = po[s2 % 2]; pok = f"rw_po{s2 % 2}"
                                for half in range(2):
                                    pb = psF[half]; pk = f"psF{half}"
                                    for hp in range(4):
                                        P.mm(pb[:, 0:512], zb[:, hp, s2 * 128:(s2 + 1) * 128], wobf[:, hp, half * 512:(half + 1) * 512],
                                             start=(hp == 0), stop=(hp == 3), r=["rw_zb", "rw_wobf"], w=[pk])
                                    P.cp('act', pob[:, half * 512:(half + 1) * 512], pb[:, 0:512], [pk], [pok])
                                b0 = bc + s2 * 128
                                prow = rb * 512 + b0 % 512
                                P.load(part_d[b0 // 512].ap()[prow:prow + 128, :], pob[:], pok + "s", r=[pok], w=[], q='pool')
                return

            sweep(0)
            P.barrier()
            if DBG >= 3:
                sweep(1)
        P.barrier()

    def mla_layer(j):
        M = ML[j]
        P.barrier()
        with ExitStack() as es:
            sb = lambda n, s, d=F32: es.enter_context(nc.sbuf_tensor(uname(n), s, d))
            psF, psB = alloc_psum(es, 4, 2)
            stg = [sb(f"stg{i}", [128, 1024]) for i in range(2)]
            wA = sb("ma_wA", [128, 8, 704], BF16)
            uqr = sb("ma_uqr", [128, 3, 1024], BF16)
            qn_t = sb("ma_qn", [128, 384]); kvn_t = sb("ma_kvn", [128, 256])
            fill_w_bf16(wA, "ma_wA", M["inA"].ap(), 8, 704, stg)
            fill_w_bf16(uqr, "ma_uqr", M["uqr"].ap(), 3, 1024, stg)
            P.load(qn_t[:], M["qn"].ap(), "ma_qn", w=["ma_qn"])
            P.load(kvn_t[:], M["kvn"].ap(), "ma_kvn", w=["ma_kvn"])
            hTt = [sb(f"ma_hT{i}", [128, 8, 128], BF16) for i in range(2)]
            rp = [sb(f"ma_rp{i}", [128, 64]) for i in range(2)]
            cA = sb("ma_cA", [128, 704]); sq = sb("ma_sq", [128, 384]); st = sb("ma_st", [128, 8])
            nb = sb("ma_nb", [128, 768], BF16)
            t1 = sb("ma_t1", [128, 32]); t2 = sb("ma_t2", [128, 32]); kr = sb("ma_kr", [128, 64])
            nT = sb("ma_nT", [128, 6, 128], BF16)
            cqS = [sb(f"ma_cqS{i}", [128, 3, 512], BF16) for i in range(2)]
            ltS = [sb(f"ma_ltS{i}", [128, 3, 512], BF16) for i in range(2)]
            qpS = [sb(f"ma_qpS{i}", [128, 8, 512], BF16) for i in range(2)]
            qf = sb("ma_qf", [128, 1024]); qa = sb("ma_qa", [128, 512]); qb = sb("ma_qb", [128, 512])
            qr = sb("ma_qr", [128, 1024], BF16)

            def bc_ap(t, c0):
                base = t[:, c0:c0 + 32]
                return bass.AP(base.tensor, base.offset, [list(base.ap[0]), [0, 16], [1, 32]])

            for t in range(TOWN // 128):
                b = t % 2
                k4 = t // 4; sub = t % 4; g4 = k4 % 2
                hk = f"ma_hT{b}"
                P.load(hTt[b][:], hT_loc[k4].ap()[:, sub * 128:(sub + 1) * 128].rearrange("(c p) t -> p c t", p=128), hk, w=[hk])
                P.load(rp[b][:], rope_d.ap()[t * 128:(t + 1) * 128, :], f"ma_rp{b}", w=[f"ma_rp{b}"])
                for half in range(2):
                    for c in range(8):
                        P.mm(psF[half][:, 0:352], hTt[b][:, c, :], wA[:, c, half * 352:(half + 1) * 352],
                             start=(c == 0), stop=(c == 7), r=[hk, "ma_wA"], w=[f"psF{half}"])
                    P.cp('act', cA[:, half * 352:(half + 1) * 352], psF[half][:, 0:352], [f"psF{half}"], ["ma_cA"])
                P.memset('dve', st[:], 0.0, w=["ma_st"])
                for (c0, n, gtile, gk, o0, col) in ((0, 384, qn_t, "ma_qn", 0, 0), (384, 256, kvn_t, "ma_kvn", 384, 4)):
                    P.act(sq[:, 0:n], cA[:, c0:c0 + n], AF.Square, accum=st[:, col:col + 1], r=["ma_cA", "ma_st"], w=["ma_sq", "ma_st"])
                    P.ts('dve', st[:, col + 1:col + 2], st[:, col:col + 1], 1.0 / n, 1e-6, ALU.mult, ALU.add, r=["ma_st"], w=["ma_st"])
                    P.act(st[:, col + 2:col + 3], st[:, col + 1:col + 2], AF.Sqrt, r=["ma_st"], w=["ma_st"])
                    P.op('dve', lambda h, col=col: h.reciprocal(st[:, col + 3:col + 4], st[:, col + 2:col + 3]), ["ma_st"], ["ma_st"])
                    P.stt('dve', nb[:, o0:o0 + n], cA[:, c0:c0 + n], st[:, col + 3:col + 4], gtile[:], ALU.mult, ALU.mult,
                          ["ma_cA", "ma_st", gk], ["ma_nb"])
                cs_, sn_ = rp[b][:, 0:32], rp[b][:, 32:64]
                rk = f"ma_rp{b}"
                P.tt('dve', t1[:], cA[:, 640:672], cs_, ALU.mult, ["ma_cA", rk], ["ma_t1"])
                P.tt('dve', t2[:], cA[:, 672:704], sn_, ALU.mult, ["ma_cA", rk], ["ma_t2"])
                P.tt('dve', kr[:, 0:32], t1[:], t2[:], ALU.subtract, ["ma_t1", "ma_t2"], ["ma_kr"])
                P.tt('dve', t1[:], cA[:, 672:704], cs_, ALU.mult, ["ma_cA", rk], ["ma_t1"])
                P.tt('dve', t2[:], cA[:, 640:672], sn_, ALU.mult, ["ma_cA", rk], ["ma_t2"])
                P.tt('dve', kr[:, 32:64], t1[:], t2[:], ALU.add, ["ma_t1", "ma_t2"], ["ma_kr"])
                P.cp('dve', nb[:, 640:704], kr[:], ["ma_kr"], ["ma_nb"])
                P.cp('dve', nb[:, 704:768], kr[:], ["ma_kr"], ["ma_nb"])
                for c in range(6):
                    P.tr(psB[0][:, c * 128:(c + 1) * 128], nb[:, c * 128:(c + 1) * 128], IDENT, ["ma_nb", "idb"], ["psB0"])
                P.cp('act', nT[:], psB[0][:, 0:768].rearrange("p (c t) -> p c t", c=6), ["psB0"], ["ma_nT"])
                P.cp('pool', cqS[g4][:, :, sub * 128:(sub + 1) * 128], nT[:, 0:3, :], ["ma_nT"], [f"ma_cqS{g4}"])
                P.cp('pool', ltS[g4][:, :, sub * 128:(sub + 1) * 128], nT[:, 3:6, :], ["ma_nT"], [f"ma_ltS{g4}"])
                for half in range(2):
                    for c in range(3):
                        P.mm(psF[2 + half][:, 0:512], nT[:, c, :], uqr[:, c, half * 512:(half + 1) * 512],
                             start=(c == 0), stop=(c == 2), r=["ma_nT", "ma_uqr"], w=[f"psF{2 + half}"])
                    P.cp('act', qf[:, half * 512:(half + 1) * 512], psF[2 + half][:, 0:512], [f"psF{2 + half}"], ["ma_qf"])
                qv = qf[:].rearrange("p (h j) -> p h j", h=16)
                qrv = qr[:].rearrange("p (h j) -> p h j", h=16)
                qav = qa[:].rearrange("p (h j) -> p h j", h=16)
                qbv = qb[:].rearrange("p (h j) -> p h j", h=16)
                cB, sB = bc_ap(rp[b], 0), bc_ap(rp[b], 32)
                P.tt('dve', qav, qv[:, :, 0:32], cB, ALU.mult, ["ma_qf", rk], ["ma_qa"])
                P.tt('pool', qbv, qv[:, :, 32:64], sB, ALU.mult, ["ma_qf", rk], ["ma_qb"])
                P.tt('dve', qrv[:, :, 0:32], qav, qbv, ALU.subtract, ["ma_qa", "ma_qb"], ["ma_qr"])
                P.tt('dve', qav, qv[:, :, 32:64], cB, ALU.mult, ["ma_qf", rk], ["ma_qa"])
                P.tt('pool', qbv, qv[:, :, 0:32], sB, ALU.mult, ["ma_qf", rk], ["ma_qb"])
                P.tt('dve', qrv[:, :, 32:64], qav, qbv, ALU.add, ["ma_qa", "ma_qb"], ["ma_qr"])
                for c in range(8):
                    P.tr(psB[1][:, c * 128:(c + 1) * 128], qr[:, c * 128:(c + 1) * 128], IDENT, ["ma_qr", "idb"], ["psB1"])
                P.cp('act', qpS[g4][:, :, sub * 128:(sub + 1) * 128], psB[1][:].rearrange("p (c t) -> p c t", c=8), ["psB1"], [f"ma_qpS{g4}"])
                if sub == 3:
                    c0 = k4 * 512
                    P.load(cqT_d.ap()[:, c0:c0 + 512].rearrange("(c p) t -> p c t", p=128), cqS[g4][:], f"ma_cqs{g4}", r=[f"ma_cqS{g4}"], w=[], q='pool')
                    P.load(lat_loc[k4].ap().rearrange("(c p) t -> p c t", p=128), ltS[g4][:], f"ma_lts{g4}", r=[f"ma_ltS{g4}"], w=[], q='pool')
                    P.load(qpeT_d.ap()[:, c0:c0 + 512].rearrange("(c p) t -> p c t", p=128), qpS[g4][:], f"ma_qps{g4}", r=[f"ma_qpS{g4}"], w=[], q='pool')
        collective("AllGather", ALU.bypass, lat_loc, lat_all, f"agl{j}")
        with ExitStack() as es:
            sb = lambda n, s, d=F32: es.enter_context(nc.sbuf_tensor(uname(n), s, d))
            psF, psB = alloc_psum(es, 8, 0)
            stg = [sb(f"stg{i}", [128, 1024]) for i in range(2)]
            ukvk = sb("mt_ukvk", [128, 2, 2048], BF16); ukvv = sb("mt_ukvv", [128, 2, 2048], BF16)
            uqn = sb("mt_uqn", [128, 3, 2048], BF16)
            fill_w_bf16(ukvk, "mt_ukvk", M["ukvk"].ap(), 2, 2048, stg)
            fill_w_bf16(ukvv, "mt_ukvv", M["ukvv"].ap(), 2, 2048, stg)
            fill_w_bf16(uqn, "mt_uqn", M["uqn"].ap(), 3, 2048, stg)
            TKMAX = 4 * TSG
            kpe = sb("mt_kpe", [128, TKMAX], BF16)
            Kh = sb("mt_Kh", [128, TKMAX], BF16)
            Vh = sb("mt_Vh", [128, TKMAX // 128, 129], BF16)
            latp = [sb(f"mt_lp{i}", [128, 2, 512], BF16) for i in range(2)]
            cqt = [sb(f"mt_cq{i}", [128, 3, 512], BF16) for i in range(2)]
            qpt = [[sb(f"mt_qp{pp}{i}", [128, 512], BF16) for i in range(2)] for pp in range(2)]
            Qn = [sb(f"mt_Qn{i}", [128, 512], BF16) for i in range(2)]
            PT = [sb(f"mt_PT{i}", [128, 512], BF16) for i in range(3)]
            osb = [sb(f"mt_os{i}", [128, 4, 128], BF16) for i in range(2)]
            rden = sb("mt_rden", [128, 4])
            P.memset('pool', Vh[:, :, 128:129], 1.0, w=["mt_Vh"])
            for pp in range(2):
                for i in range(2):
                    P.memset('pool', qpt[pp][i][:], 0.0, w=[f"mt_qp{pp}{i}"])
            step = [0]
            for (Tk, q0, nq, pieces) in (
                (TPR, 0, TPR, [(lat_loc[p], 0, p * 512) for p in range(TPR // 512)]),
                (4 * TSG, TPR, TSG, [(lat_all[4 + (p % 8)], (p // 8) * 384, p * 512) for p in range(4 * TSG // 512)]),
            ):
                for (src, r0, kc) in pieces:
                    P.load(kpe[:, kc:kc + 512], src.ap()[r0 + 256:r0 + 384, :], "mt_kpe", w=["mt_kpe"])
                for h in range(16):
                    hc = slice(h * 128, (h + 1) * 128)
                    for pi, (src, r0, kc) in enumerate(pieces):
                        lb = pi % 2; lk = f"mt_lp{lb}"
                        P.load(latp[lb][:], src.ap()[r0:r0 + 256, :].rearrange("(c p) t -> p c t", p=128), lk, w=[lk])
                        for c in range(2):
                            P.mm(psF[6][:, 0:512], ukvk[:, c, hc], latp[lb][:, c, :], start=(c == 0), stop=(c == 1), r=["mt_ukvk", lk], w=["psF6"])
                        P.cp('dve', Kh[:, kc:kc + 512], psF[6][:, 0:512], ["psF6"], ["mt_Kh"])
                        for kt in range(4):
                            for c in range(2):
                                P.mm(psF[7][:, kt * 128:(kt + 1) * 128], latp[lb][:, c, kt * 128:(kt + 1) * 128], ukvv[:, c, hc],
                                     start=(c == 0), stop=(c == 1), r=[lk, "mt_ukvv"], w=["psF7"])
                        P.cp('act', Vh[:, kc // 128:kc // 128 + 4, 0:128], psF[7][:, 0:512].rearrange("p (a v) -> p a v", a=4), ["psF7"], ["mt_Vh"])
                    hl = slice((h % 2) * 64, (h % 2) * 64 + 64)
                    for qt in range(nq // 512):
                        qb_ = qt % 2
                        qc = q0 + qt * 512
                        P.load(cqt[qb_][:], cqT_d.ap()[:, qc:qc + 512].rearrange("(c p) t -> p c t", p=128), f"mt_cq{qb_}", w=[f"mt_cq{qb_}"])
                        par = h % 2
                        qpk = f"mt_qp{par}{qb_}"
                        P.load(qpt[par][qb_][hl, :], qpeT_d.ap()[(h // 2) * 128 + par * 64:(h // 2) * 128 + par * 64 + 64, qc:qc + 512], qpk, w=[qpk])
                        for c in range(3):
                            P.mm(psF[6][:, 0:512], uqn[:, c, hc], cqt[qb_][:, c, :], start=(c == 0), stop=(c == 2), r=["mt_uqn", f"mt_cq{qb_}"], w=["psF6"])
                        P.cp('dve', Qn[qb_][:], psF[6][:, 0:512], ["psF6"], [f"mt_Qn{qb_}"])
                        nkt = Tk // 128
                        for kt in range(nkt):
                            sp = 4 + (step[0] % 2); spk = f"psF{sp}"
                            pb3 = step[0] % 3
                            step[0] += 1
                            ks = slice(kt * 128, (kt + 1) * 128)
                            P.mm(psF[sp][:, 0:512], Kh[:, ks], Qn[qb_][:], start=True, stop=False, r=["mt_Kh", f"mt_Qn{qb_}"], w=[spk])
                            P.mm(psF[sp][:, 0:512], kpe[:, ks], qpt[par][qb_][:, :], start=False, stop=True, r=["mt_kpe", qpk], w=[spk])
                            P.act(PT[pb3][:], psF[sp][:, 0:512], AF.Exp, scale=QK_SCALE, r=[spk], w=[f"mt_PT{pb3}"])
                            for qs in range(4):
                                P.mm(psF[qs][:, 0:129], PT[pb3][:, qs * 128:(qs + 1) * 128], Vh[:, kt, :], start=(kt == 0), stop=(kt == nkt - 1),
                                     r=[f"mt_PT{pb3}", "mt_Vh"], w=[f"psF{qs}"])
                        ob = qt % 2
                        for qs in range(4):
                            P.op('dve', lambda h_, qs=qs: h_.reciprocal(rden[:, qs:qs + 1], psF[qs][:, 128:129]), [f"psF{qs}"], ["mt_rden"])
                            P.ts('dve', osb[ob][:, qs, :], psF[qs][:, 0:128], rden[:, qs:qs + 1], None, ALU.mult, r=[f"psF{qs}", "mt_rden"], w=[f"mt_os{ob}"])
                        P.load(o_scr.ap()[qc:qc + 512, hc].rearrange("(s p) v -> p s v", p=128), osb[ob][:], f"mt_oss{ob}", r=[f"mt_os{ob}"], w=[], q='pool')
        P.barrier()
        with ExitStack() as es:
            sb = lambda n, s, d=F32: es.enter_context(nc.sbuf_tensor(uname(n), s, d))
            psF, psB = alloc_psum(es, 6, 2)
            stg = [sb(f"stg{i}", [128, 1024]) for i in range(2)]
            wG = sb("me_wG", [128, 8, E], BF16); wO = sb("me_wO", [128, 16, D], BF16)
            fill_w_bf16(wG, "me_wG", M["inG"].ap(), 8, E, stg)
            fill_w_bf16(wO, "me_wO", M["out"].ap(), 16, D, stg)
            hTt = [sb(f"me_hT{i}", [128, 8, 128], BF16) for i in range(2)]
            ot = [sb(f"me_o{i}", [128, E], BF16) for i in range(2)]
            xt = [sb(f"me_x{i}", [128, D]) for i in range(2)]
            sg = sb("me_sg", [128, E]); zt = sb("me_z", [128, E], BF16); zT = sb("me_zT", [128, 16, 128], BF16)
            for t in range(TOWN // 128):
                b = t % 2
                rows = slice(t * 128, (t + 1) * 128)
                P.load(hTt[b][:], hT_loc[t // 4].ap()[:, (t % 4) * 128:(t % 4 + 1) * 128].rearrange("(c p) t -> p c t", p=128), f"me_hT{b}", w=[f"me_hT{b}"])
                P.load(ot[b][:], o_scr.ap()[rows, :], f"me_o{b}", w=[f"me_o{b}"])
                P.load(xt[b][:], x_own.ap()[rows, :], f"me_x{b}", w=[f"me_x{b}"])
                for q in range(4):
                    for c in range(8):
                        P.mm(psF[q][:, 0:512], hTt[b][:, c, :], wG[:, c, q * 512:(q + 1) * 512], start=(c == 0), stop=(c == 7),
                             r=[f"me_hT{b}", "me_wG"], w=[f"psF{q}"])
                    P.act(sg[:, q * 512:(q + 1) * 512], psF[q][:, 0:512], AF.Silu, r=[f"psF{q}"], w=["me_sg"])
                P.tt('dve', zt[:], sg[:], ot[b][:], ALU.mult, ["me_sg", f"me_o{b}"], ["me_z"])
                for hf in range(2):
                    for c in range(8):
                        e = hf * 8 + c
                        P.tr(psB[hf][:, c * 128:(c + 1) * 128], zt[:, e * 128:(e + 1) * 128], IDENT, ["me_z", "idb"], [f"psB{hf}"])
                    P.cp('act' if hf == 0 else 'dve', zT[:, hf * 8:(hf + 1) * 8, :], psB[hf][:].rearrange("p (c t) -> p c t", c=8), [f"psB{hf}"], ["me_zT"])
                for half in range(2):
                    for e in range(16):
                        P.mm(psF[4 + half][:, 0:512], zT[:, e, :], wO[:, e, half * 512:(half + 1) * 512], start=(e == 0), stop=(e == 15),
                             r=["me_zT", "me_wO"], w=[f"psF{4 + half}"])
                    P.tt('dve', xt[b][:, half * 512:(half + 1) * 512], xt[b][:, half * 512:(half + 1) * 512], psF[4 + half][:, 0:512], ALU.add,
                         [f"me_x{b}", f"psF{4 + half}"], [f"me_x{b}"])
                P.load(x_own.ap()[rows, :], xt[b][:], f"me_xs{b}", r=[f"me_x{b}"], w=[], q='pool')
        P.barrier()

    xcur = xin
    pend_delta = None
    for li in range(nlayers):
        jj = li // 2
        if li % 2 == 0:
            norm_phase(xcur, None, None, li)
            if DBG >= 1:
                collective("AllGather", ALU.bypass, hT_loc, hT_all, f"agh{li}")
            if DBG >= 2:
                rwkv_layer(jj)
            if DBG >= 4:
                collective("ReduceScatter", ALU.add, part_d, delta_d, f"rs{li}")
                pend_delta = delta_d
        else:
            norm_phase(xcur, pend_delta, x_own, li)
            pend_delta = None
            xcur = x_own
            mla_layer(jj)
    if pend_delta is not None:
        norm_phase(xcur, pend_delta, x_own, 4, final=True)
    else:
        norm_phase(xcur, None, None, 4, final=True)
    P.emit()
    return nc


def _consts():
    c = np.zeros((128, 768), np.float32)
    i = np.arange(128)[:, None]
    t = np.arange(128)[None, :]
    c[:, 0:128] = (i == t)
    c[:, 128:256] = (i < t)
    c[:, 256:384] = (i <= t)
    c[:, 384:512] = (i > t)
    c[:, 512:640] = (i >= t)
    c[:, 640:768] = ((i // 64) == (t // 64))
    return c


def _rope_table(pos):
    inv_freq = (1.0 / (np.float32(10000.0) ** (np.arange(0, 64, 2, dtype=np.float32) / np.float32(64)))).astype(np.float32)
    ang = (pos.astype(np.float32)[:, None] * inv_freq[None, :]).astype(np.float32)
    return np.concatenate([np.cos(ang), np.sin(ang)], axis=1).astype(np.float32)


_NC_CACHE = {}


def make_in_maps(inp):
    f = lambda a: np.ascontiguousarray(np.asarray(a, dtype=np.float32))
    xp, xs = f(inp["x_prompt"]), f(inp["x_sample"])
    consts = _consts()
    lng = np.concatenate([f(inp["ln_g"]), f(inp["final_g"])[None, :]], axis=0)
    lng = np.ascontiguousarray(np.broadcast_to(lng[:, None, :], (5, 128, D)))
    rw_in, rw_out = f(inp["rw_in"]), f(inp["rw_out"])
    maps = []
    for c in range(8):
        g, r = c // 4, c % 4
        m = {}
        m["xin"] = np.ascontiguousarray(np.concatenate([xp[c], xs[g, r * TSG:(r + 1) * TSG]], axis=0))
        m["consts"] = consts
        pos = np.concatenate([np.arange(TPR), np.arange(r * TSG, (r + 1) * TSG)])
        m["rope"] = _rope_table(pos)
        m["lng"] = lng
        my = slice(r * 512, (r + 1) * 512)
        for j in range(2):
            w = rw_in[j]
            cols = [w[:, s * E:(s + 1) * E][:, my] for s in range(4)] + [w[:, 4 * E:4 * E + 128], w[:, 4 * E + 128:4 * E + 256]]
            m[f"rw_wloc{j}"] = np.ascontiguousarray(np.concatenate(cols, axis=1))
            m[f"rw_w2{j}"] = np.ascontiguousarray(f(inp["rw_w2"])[j].reshape(128, E)[:, my])
            m[f"rw_a2{j}"] = np.ascontiguousarray(f(inp["rw_a2"])[j].reshape(128, E)[:, my])
            m[f"rw_wout{j}"] = np.ascontiguousarray(rw_out[j][my, :])
            vec = np.zeros((128, NVEC), np.float32)
            mu = f(inp["rw_mu"])[j]
            vec[:, 0:48] = mu.reshape(6, 8, 128).transpose(2, 1, 0).reshape(128, 48)
            def pc(v):
                return v[my].reshape(4, 128).T
            vec[:, 48:52] = pc(f(inp["rw_w0"])[j, 0]); vec[:, 52:56] = pc(f(inp["rw_w0"])[j, 1])
            vec[:, 56:60] = pc(f(inp["rw_a0"])[j, 0]); vec[:, 60:64] = pc(f(inp["rw_a0"])[j, 1])
            vec[:, 64:68] = pc(f(inp["rw_kk"])[j]); vec[:, 68:72] = pc(f(inp["rw_ka"])[j])
            vec[:, 72:76] = pc(f(inp["rw_rk"])[j].reshape(E))
            vec[:, 76:80] = pc(f(inp["rw_lnx_g"])[j]); vec[:, 80:84] = pc(f(inp["rw_lnx_b"])[j])
            m[f"rw_vec{j}"] = vec
            mi = f(inp["ml_in"])[j]
            m[f"ml_inA{j}"] = np.ascontiguousarray(mi[:, 0:704])
            m[f"ml_inG{j}"] = np.ascontiguousarray(mi[:, 704:704 + E])
            uq = f(inp["ml_uq"])[j].reshape(384, 16, 192)
            m[f"ml_uqn{j}"] = np.ascontiguousarray(uq[:, :, 0:128].reshape(384, 2048))
            m[f"ml_uqr{j}"] = np.ascontiguousarray(uq[:, :, 128:192].reshape(384, 1024))
            ukv = f(inp["ml_ukv"])[j].reshape(256, 16, 256)
            m[f"ml_ukvk{j}"] = np.ascontiguousarray(ukv[:, :, 0:128].reshape(256, 2048))
            m[f"ml_ukvv{j}"] = np.ascontiguousarray(ukv[:, :, 128:256].reshape(256, 2048))
            m[f"ml_out{j}"] = f(inp["ml_out"])[j]
            m[f"ml_qn{j}"] = np.ascontiguousarray(np.broadcast_to(f(inp["ml_qn"])[j][None, :], (128, 384)))
            m[f"ml_kvn{j}"] = np.ascontiguousarray(np.broadcast_to(f(inp["ml_kvn"])[j][None, :], (128, 256)))
        maps.append(m)
    return maps


def run(inp, nlayers=4, trace=False):
    if nlayers not in _NC_CACHE:
        _NC_CACHE[nlayers] = build(nlayers)
    nc = _NC_CACHE[nlayers]
    maps = make_in_maps(inp)
    res = run_bass_kernel_spmd(nc, maps, core_ids=list(range(8)), trace=trace) if trace else \
        run_bass_kernel_spmd(nc, maps, core_ids=list(range(8)))
    yp = np.zeros((8, TPR, D), np.float32)
    ys = np.zeros((2, 4 * TSG, D), np.float32)
    for c in range(8):
        y = np.asarray(res.results[c]["y"], dtype=np.float32)
        yp[c] = y[0:TPR]
        ys[c // 4, (c % 4) * TSG:(c % 4 + 1) * TSG] = y[TPR:]
    return yp, ys, res


def kernel(**inputs):
    yp, ys, _ = run(inputs, 4)
    return (yp, ys)
```

```python
import numpy as np
import ml_dtypes
import concourse.bass as bass
import concourse.mybir as mybir
from concourse.bass_utils import run_bass_kernel_spmd
from contextlib import ExitStack

F32 = mybir.dt.float32
BF16 = mybir.dt.bfloat16
AF = mybir.ActivationFunctionType
ALU = mybir.AluOpType
AX = mybir.AxisListType
class Prog:
    ENG = ('pe', 'act', 'dve', 'pool', 'sp')

    def __init__(self, nc):
        self.nc = nc
        self.streams = {e: [] for e in self.ENG}
        self.sems = {}
        self.cnt = {}
        self.waited = {e: {} for e in self.ENG}
        self.last_w = {}
        self.readers = {}
        self.nops = 0

    def sem(self, key):
        if key not in self.sems:
            self.sems[key] = self.nc.alloc_semaphore("s_" + key.replace(':', '_'))
            self.cnt[key] = 0
        return self.sems[key]

    def _need(self, eng, toks):
        best = {}
        w = self.waited[eng]
        for t in toks:
            if t is None:
                continue
            k, v = t
            if w.get(k, 0) >= v:
                continue
            if best.get(k, 0) < v:
                best[k] = v
        for k, v in best.items():
            w[k] = v
        return list(best.items())

    def _deps(self, reads, writes):
        toks = []
        for k in reads:
            toks.append(self.last_w.get(k))
        for k in writes:
            toks.append(self.last_w.get(k))
            r = self.readers.get(k)
            if r:
                toks.extend(r.items())
        return toks

    def _commit(self, tok, reads, writes):
        for k in reads:
            r = self.readers.setdefault(k, {})
            if r.get(tok[0], 0) < tok[1]:
                r[tok[0]] = tok[1]
        for k in writes:
            self.last_w[k] = tok
            self.readers[k] = {}

    def op(self, eng, fn, reads=(), writes=()):
        semkey = 'E:' + eng
        self.sem(semkey)
        toks = self._deps(reads, writes)
        if eng == 'pe':
            toks = [t for t in toks if t is not None and t[0] != semkey]
        waits = self._need(eng, toks)
        self.cnt[semkey] += 1
        tok = (semkey, self.cnt[semkey])
        self.streams[eng].append((waits, fn, semkey, 1))
        self._commit(tok, reads, writes)
        self.nops += 1
        return tok

    def dma(self, q, fn, slot, reads=(), writes=(), inc=16):
        semkey = 'D:' + slot
        self.sem(semkey)
        toks = self._deps(reads, writes)
        waits = self._need(q, toks)
        self.cnt[semkey] += inc
        tok = (semkey, self.cnt[semkey])
        self.streams[q].append((waits, fn, semkey, inc))
        self._commit(tok, reads, writes)
        self.nops += 1
        return tok

    def barrier(self):
        toks = [(k, v) for k, v in self.cnt.items() if v > 0]
        for e in self.ENG:
            waits = self._need(e, toks)
            if waits:
                self.streams[e].append((waits, None, None, 0))
        self.last_w.clear()
        self.readers.clear()

    def emit(self):
        nc = self.nc
        self.barrier()
        with nc.Block() as block:
            def run(h, name):
                sems = self.sems
                for waits, fn, semkey, inc in self.streams[name]:
                    for k, v in waits:
                        h.wait_ge(sems[k], v)
                    if fn is not None:
                        fn(h).then_inc(sems[semkey], inc)

            @block.tensor
            def _(h):
                run(h, 'pe')

            @block.scalar
            def _(h):
                run(h, 'act')

            @block.vector
            def _(h):
                run(h, 'dve')

            @block.gpsimd
            def _(h):
                run(h, 'pool')

            @block.sync
            def _(h):
                run(h, 'sp')

    def mm(self, out, lhsT, rhs, start=True, stop=True, r=(), w=()):
        return self.op('pe', lambda h: h.matmul(out, lhsT=lhsT, rhs=rhs, start=start, stop=stop), r, w)

    def tr(self, out, in_, ident, r=(), w=()):
        return self.op('pe', lambda h: h.transpose(out, in_, ident), r, w)

    def act(self, out, in_, func, bias=None, scale=1.0, accum=None, r=(), w=()):
        kw = {}
        if bias is not None:
            kw['bias'] = bias
        if accum is not None:
            kw['accum_out'] = accum
        return self.op('act', lambda h: h.activation(out, in_, func, scale=scale, **kw), r, w)

    def tt(self, eng, out, in0, in1, op, r=(), w=()):
        return self.op(eng, lambda h: h.tensor_tensor(out, in0, in1, op), r, w)

    def ts(self, eng, out, in0, s1, s2, op0, op1=None, r=(), w=()):
        if op1 is None:
            if op0 == ALU.mult:
                return self.op(eng, lambda h: h.tensor_scalar_mul(out, in0, s1), r, w)
            if op0 == ALU.add:
                return self.op(eng, lambda h: h.tensor_scalar_add(out, in0, s1), r, w)
            if op0 == ALU.max:
                return self.op(eng, lambda h: h.tensor_scalar_max(out, in0, s1), r, w)
            raise ValueError(op0)
        return self.op(eng, lambda h: h.tensor_scalar(out, in0, s1, s2, op0, op1), r, w)

    def stt(self, eng, out, in0, scalar, in1, op0, op1, r=(), w=()):
        return self.op(eng, lambda h: h.scalar_tensor_tensor(out, in0, scalar, in1, op0, op1), r, w)

    def cp(self, eng, out, in_, r=(), w=()):
        if eng == 'act':
            return self.op('act', lambda h: h.copy(out, in_), r, w)
        return self.op(eng, lambda h: h.tensor_copy(out, in_), r, w)

    def memset(self, eng, ap, val, w=()):
        return self.op(eng, lambda h: h.memset(ap, val), (), w)

    def load(self, out, in_, slot, r=(), w=(), q='sp', slow=False):
        if slow:
            return self.dma(q, lambda h: h.dma_start(out=out, in_=in_, allow_slow_non_contiguous=True), slot, r, w)
        return self.dma(q, lambda h: h.dma_start(out=out, in_=in_), slot, r, w)

D = 1024
E = 2048
TPR = 2048
TSG = 4096
TOWN = TPR + TSG
TG = 4 * TOWN
NVEC = 96
RT = 256
CH = 128
QK_SCALE = 192.0 ** -0.5
NEG_E05 = -0.6065306597126334


def build(nlayers=4, DBG=9, SUB=9):
    import os
    K5 = int(os.environ.get('K5', '9'))
    K7 = int(os.environ.get('K7', '9'))
    K8 = int(os.environ.get('K8', '9'))
    NSEQ = int(os.environ.get('NSEQ', '99'))
    NTL = int(os.environ.get('NTL', '9999'))
    nc = bass.Bass("TRN2", target_bir_lowering=False)
    P = Prog(nc)

    def din(name, shape, dt=F32):
        return nc.dram_tensor(name, list(shape), dt, kind="ExternalInput")

    xin = din("xin", [TOWN, D])
    consts_d = din("consts", [128, 768])
    rope_d = din("rope", [TOWN, 64])
    lng_d = din("lng", [5, 128, D])
    RW = []
    ML = []
    for j in range(2):
        RW.append(dict(wloc=din(f"rw_wloc{j}", [D, 2304]), w2=din(f"rw_w2{j}", [128, 512]),
                       a2=din(f"rw_a2{j}", [128, 512]), wout=din(f"rw_wout{j}", [512, D]),
                       vec=din(f"rw_vec{j}", [128, NVEC])))
        ML.append(dict(inA=din(f"ml_inA{j}", [D, 704]), inG=din(f"ml_inG{j}", [D, E]),
                       uqn=din(f"ml_uqn{j}", [384, 2048]), uqr=din(f"ml_uqr{j}", [384, 1024]),
                       ukvk=din(f"ml_ukvk{j}", [256, 2048]), ukvv=din(f"ml_ukvv{j}", [256, 2048]),
                       out=din(f"ml_out{j}", [E, D]), qn=din(f"ml_qn{j}", [128, 384]),
                       kvn=din(f"ml_kvn{j}", [128, 256])))
    yout = nc.dram_tensor("y", [TOWN, D], F32, kind="ExternalOutput")

    x_own = nc.dram_tensor("x_own", [TOWN, D], F32)
    NCK = TOWN // 512
    hT_loc = [nc.dram_tensor(f"hT_loc{k}", [D, 512], BF16) for k in range(NCK)]
    hT_all = [nc.dram_tensor(f"hT_all{k}", [4 * D, 512], BF16) for k in range(NCK)]
    yf_d = nc.dram_tensor("yf_d", [512, TG], F32)
    part_d = [nc.dram_tensor(f"part_d{k}", [4 * 512, D], F32) for k in range(NCK)]
    delta_d = [nc.dram_tensor(f"delta_d{k}", [512, D], F32) for k in range(NCK)]
    lat_loc = [nc.dram_tensor(f"lat_loc{k}", [384, 512], BF16) for k in range(NCK)]
    lat_all = [nc.dram_tensor(f"lat_all{k}", [4 * 384, 512], BF16) for k in range(NCK)]
    cqT_d = nc.dram_tensor("cqT_d", [384, TOWN], BF16)
    qpeT_d = nc.dram_tensor("qpeT_d", [1024, TOWN], BF16)
    o_scr = nc.dram_tensor("o_scr", [TOWN, E], BF16)
    groups = [[0, 1, 2, 3], [4, 5, 6, 7]]

    _uid = [0]

    def uname(n):
        _uid[0] += 1
        return f"{n}_u{_uid[0]}"

    cst = nc.alloc_sbuf_tensor("cst", [128, 768], F32)
    idb = nc.alloc_sbuf_tensor("idb", [128, 128], BF16)
    def alloc_psum(es, nf, nb):
        pf = [es.enter_context(nc.psum_tensor(uname(f"psF{i}"), [128, 512], F32)) for i in range(nf)]
        pb = [es.enter_context(nc.psum_tensor(uname(f"psB{i}"), [128, 1024], BF16)) for i in range(nb)]
        return pf, pb
    P.load(cst[:], consts_d.ap(), "cst", w=["cst"])
    P.cp('dve', idb[:], cst[:, 0:128], ["cst"], ["idb"])
    IDENT = idb[:]
    MF = cst[:, 128:384]
    MB = cst[:, 384:640]
    BONES = cst[:, 640:768]


    def load_w_bf16(es, name, src, nchunk, ncols, rows=128):
        dst = es.enter_context(nc.sbuf_tensor(name, [128, nchunk, ncols], BF16))
        return dst

    def fill_w_bf16(dst, name, src, nchunk, ncols, stg, rows=128):
        for c in range(nchunk):
            for c0 in range(0, ncols, 1024):
                cw = min(1024, ncols - c0)
                b = (c + c0 // 1024) % 2
                P.load(stg[b][:rows, :cw], src[c * rows:(c + 1) * rows, c0:c0 + cw], f"stg{b}", w=[f"stg{b}"])
                eng = 'pool' if b == 0 else 'act'
                P.cp(eng, dst[:rows, c, c0:c0 + cw], stg[b][:rows, :cw], [f"stg{b}"], [name])

    def norm_phase(xsrc, delta, xdst, gidx, final=False):
        P.barrier()
        with ExitStack() as es:
            sb = lambda n, s, d=F32: es.enter_context(nc.sbuf_tensor(uname(n), s, d))
            psF, psB = alloc_psum(es, 0, 2)
            gt = sb("np_g", [128, D])
            xt = [sb(f"np_x{i}", [128, D]) for i in range(2)]
            dl = [sb(f"np_d{i}", [128, D]) for i in range(2)]
            sq = sb("np_sq", [128, D])
            st = [sb(f"np_s{i}", [128, 4]) for i in range(2)]
            hb = [sb(f"np_h{i}", [128, D], BF16) for i in range(2)]
            hf = [sb(f"np_hf{i}", [128, D]) for i in range(2)]
            hTt = [sb(f"np_hT{i}", [128, 8, 512], BF16) for i in range(2)]
            P.load(gt[:], lng_d.ap()[gidx], "np_g", w=["np_g"])
            nt = TOWN // 128
            for t in range(nt):
                b = t % 2
                rows = slice(t * 128, (t + 1) * 128)
                P.load(xt[b][:], xsrc.ap()[rows, :], f"np_x{b}", w=[f"np_x{b}"])
                if delta is not None:
                    P.load(dl[b][:], delta[t // 4].ap()[(t % 4) * 128:(t % 4 + 1) * 128, :], f"np_d{b}", w=[f"np_d{b}"])
                    P.tt('pool', xt[b][:], xt[b][:], dl[b][:], ALU.add, [f"np_x{b}", f"np_d{b}"], [f"np_x{b}"])
                    P.load(xdst.ap()[rows, :], xt[b][:], f"np_xs{b}", r=[f"np_x{b}"], w=[], q='pool')
                P.memset('dve', st[b][:], 0.0, w=[f"np_s{b}"])
                P.act(sq[:], xt[b][:], AF.Square, accum=st[b][:, 0:1], r=[f"np_x{b}", f"np_s{b}"], w=["np_sq", f"np_s{b}"])
                P.ts('dve', st[b][:, 1:2], st[b][:, 0:1], 1.0 / D, 1e-6, ALU.mult, ALU.add, r=[f"np_s{b}"], w=[f"np_s{b}"])
                P.act(st[b][:, 2:3], st[b][:, 1:2], AF.Sqrt, r=[f"np_s{b}"], w=[f"np_s{b}"])
                P.op('dve', lambda h, b=b: h.reciprocal(st[b][:, 3:4], st[b][:, 2:3]), [f"np_s{b}"], [f"np_s{b}"])
                if final:
                    P.stt('dve', hf[b][:], xt[b][:], st[b][:, 3:4], gt[:], ALU.mult, ALU.mult,
                          r=[f"np_x{b}", f"np_s{b}", "np_g"], w=[f"np_hf{b}"])
                    P.load(yout.ap()[rows, :], hf[b][:], f"np_ys{b}", r=[f"np_hf{b}"], w=[], q='pool')
                    continue
                P.stt('dve', hb[b][:], xt[b][:], st[b][:, 3:4], gt[:], ALU.mult, ALU.mult,
                      r=[f"np_x{b}", f"np_s{b}", "np_g"], w=[f"np_h{b}"])
                pt = psB[b]
                for c in range(8):
                    P.tr(pt[:, c * 128:(c + 1) * 128], hb[b][:, c * 128:(c + 1) * 128], IDENT, [f"np_h{b}", "idb"], [f"psB{b}"])
                g4 = (t // 4) % 2
                sub = t % 4
                P.cp('act', hTt[g4][:, :, sub * 128:(sub + 1) * 128], pt[:].rearrange("p (c t) -> p c t", c=8),
                     [f"psB{b}"], [f"np_hT{g4}"])
                if sub == 3:
                    P.load(hT_loc[t // 4].ap().rearrange("(c p) t -> p c t", p=128), hTt[g4][:],
                           f"np_hTs{g4}", r=[f"np_hT{g4}"], w=[], q='pool')
        P.barrier()

    def collective(kind, op, srcs, dsts, name):
        P.barrier()
        for k, (src, dst) in enumerate(zip(srcs, dsts)):
            P.dma('pool', lambda h, src=src, dst=dst: h.collective_compute(kind, op, replica_groups=groups,
                                                                         ins=[src.ap().opt()], outs=[dst.ap().opt()]),
                  f"cc_{name}_{k % 4}", [], [], inc=1)
        P.barrier()

    def seq_list():
        seqs = [[(rb, 0, TPR)] for rb in range(4)]
        seqs.append([(rb, TPR, TSG) for rb in range(4)])
        return seqs

    def rwkv_layer(j):
        W = RW[j]
        P.barrier()
        with ExitStack() as es:
            sb = lambda n, s, d=F32: es.enter_context(nc.sbuf_tensor(uname(n), s, d))
            psF, psB = alloc_psum(es, 7, 1)
            wbf = sb("rw_wbf", [128, 8, 2304], BF16)
            w2bf = sb("rw_w2bf", [128, 1, 512], BF16)
            a2bf = sb("rw_a2bf", [128, 1, 512], BF16)
            wobf = sb("rw_wobf", [128, 4, D], BF16)
            vec = sb("rw_vec", [128, NVEC])
            stg = [sb(f"stg{i}", [128, 1024]) for i in range(2)]
            fill_w_bf16(wbf, "rw_wbf", W["wloc"].ap(), 8, 2304, stg)
            fill_w_bf16(w2bf, "rw_w2bf", W["w2"].ap(), 1, 512, stg)
            fill_w_bf16(a2bf, "rw_a2bf", W["a2"].ap(), 1, 512, stg)
            fill_w_bf16(wobf, "rw_wobf", W["wout"].ap(), 4, D, stg)
            P.load(vec[:], W["vec"].ap(), "rw_vec", w=["rw_vec"])
            P.ts('dve', vec[:, 84:88], vec[:, 68:72], -1.0, 1.0, ALU.mult, ALU.add, r=["rw_vec"], w=["rw_vec"])
            P.ts('dve', vec[:, 88:92], vec[:, 68:72], -2.0, 2.0, ALU.mult, ALU.add, r=["rw_vec"], w=["rw_vec"])
            VW0 = lambda d, hp: vec[:, 48 + 4 * d + hp:49 + 4 * d + hp]
            VA0 = lambda d, hp: vec[:, 56 + 4 * d + hp:57 + 4 * d + hp]
            VKK = lambda hp: vec[:, 64 + hp:65 + hp]
            VKA = lambda hp: vec[:, 68 + hp:69 + hp]
            VRK = lambda hp: vec[:, 72 + hp:73 + hp]
            VLG = lambda hp: vec[:, 76 + hp:77 + hp]
            VLB = lambda hp: vec[:, 80 + hp:81 + hp]
            VOM = lambda hp: vec[:, 84 + hp:85 + hp]
            VOM2 = lambda hp: vec[:, 88 + hp:89 + hp]
            VMU = lambda c, s: vec[:, c * 6 + s:c * 6 + s + 1]

            hx = [sb(f"rw_hx{i}", [128, 8, RT + 2], BF16) for i in range(2)]
            xx = sb("rw_xx", [128, 8, RT])
            lerp = [sb(f"rw_lp{i}", [128, 8, RT], BF16) for i in range(2)]
            pr = {s: [sb(f"rw_p{s}{hp}", [128, RT]) for hp in range(4)] for s in "rkvg"}
            wlt = [sb(f"rw_wl{d}", [128, RT], BF16) for d in range(2)]
            alt = [sb(f"rw_al{d}", [128, RT], BF16) for d in range(2)]
            kk = sb("rw_kk", [128, RT]); kk2 = sb("rw_kk2", [128, RT]); kkn = sb("rw_kkn", [128, RT])
            lw = sb("rw_lw", [128, RT]); av = [sb(f"rw_a{d}", [128, RT]) for d in range(2)]
            tmp = sb("rw_tmp", [128, RT]); kd = sb("rw_kd", [128, RT]); bb = sb("rw_b", [128, RT])
            cum = sb("rw_cum", [128, RT]); ex = sb("rw_ex", [128, RT]); ones = sb("rw_ones", [128, RT])
            krt = sb("rw_krt", [128, 2, 2, CH], BF16)
            bhb = sb("rw_bhb", [128, RT], BF16)
            khbZ = [sb(f"rw_khbZ{i}", [128, RT], BF16) for i in range(2)]
            bhbZ = [sb(f"rw_bhbZ{i}", [128, RT], BF16) for i in range(2)]
            kktZ = [sb(f"rw_kktZ{i}", [128, 2, CH], BF16) for i in range(2)]
            Kbb = sb("rw_Kbb", [128, RT], BF16); Bbb = sb("rw_Bbb", [128, RT], BF16)
            vbf = sb("rw_vbf", [128, RT], BF16)
            gC = sb("rw_gC", [128, 2])
            tokm = [sb(f"rw_tokm{i}", [128, 4, CH], BF16) for i in range(2)]
            vpad = [sb(f"rw_vpad{i}", [128, 2, CH], BF16) for i in range(2)]
            upad = [sb(f"rw_upad{i}", [128, 2, CH], BF16) for i in range(2)]
            ucat = [sb(f"rw_ucat{i}", [128, CH], BF16) for i in range(2)]
            AbT = [sb(f"rw_AbT{i}", [128, 2 * CH], BF16) for i in range(2)]
            AkT = [sb(f"rw_AkT{i}", [128, 2 * CH], BF16) for i in range(2)]
            Gp = [sb(f"rw_G{i}", [128, 2, CH], BF16) for i in range(2)]
            TTp = [sb(f"rw_TT{i}", [128, CH], BF16) for i in range(2)]
            TTh = [sb(f"rw_TTh{i}", [128, CH], BF16) for i in range(2)]
            WTb = sb("rw_WT", [128, CH], BF16)
            AVb = sb("rw_AV", [128, 2, 64], BF16)
            H32 = [sb(f"rw_H32{hp}", [128, CH]) for hp in range(4)]
            Hbf = [sb(f"rw_Hbf{hp}", [128, CH], BF16) for hp in range(4)]
            htmp = sb("rw_htmp", [128, CH])
            yT = [sb(f"rw_yT{hp}", [128, RT]) for hp in range(4)]
            yfl = [sb(f"rw_yfl{hp}", [128, RT]) for hp in range(4)]
            zb = sb("rw_zb", [128, 4, RT], BF16)
            po = [sb(f"rw_po{i}", [128, D]) for i in range(2)]
            q1 = sb("rw_q1", [128, RT]); q2 = sb("rw_q2", [128, RT]); q3 = sb("rw_q3", [128, RT])

            P.memset('pool', ones[:], 1.0, w=["rw_ones"])
            P.memset('pool', WTb[:], 0.0, w=["rw_WT"])
            for i in range(2):
                P.memset('pool', khbZ[i][:], 0.0, w=[f"rw_khbZ{i}"])
                P.memset('pool', bhbZ[i][:], 0.0, w=[f"rw_bhbZ{i}"])
                P.memset('pool', kktZ[i][:], 0.0, w=[f"rw_kktZ{i}"])
                P.memset('pool', wlt[i][:], 0.0, w=[f"rw_wl{i}"])
                P.memset('pool', alt[i][:], 0.0, w=[f"rw_al{i}"])
            for i in range(2):
                P.memset('pool', vpad[i][:], 0.0, w=[f"rw_vpad{i}"])
                P.memset('pool', upad[i][:], 0.0, w=[f"rw_upad{i}"])

            def diag_ap(t):
                base = t[:]
                return bass.AP(base.tensor, base.offset, [list(base.ap[0]), [CH + 64, 2], [1, 64]])

            chunk_ctr = [0]

            def sweep(d):
                MA = MF if d == 0 else MB
                MP = MB[:, 0:128] if d == 0 else MF[:, 0:128]
                for seq in seq_list()[:NSEQ]:
                    tiles = []
                    for pi, (rb, c0, n) in enumerate(seq):
                        for t0 in range(0, n, RT):
                            tiles.append((pi, rb, c0 + t0))
                    ntl = len(tiles)
                    order = list(range(ntl) if d == 0 else range(ntl - 1, -1, -1))[:NTL]
                    for hp in range(4):
                        P.memset('pool', H32[hp][:], 0.0, w=[f"rw_H32{hp}"])
                        P.memset('pool', Hbf[hp][:], 0.0, w=[f"rw_Hbf{hp}"])
                    for ti in order:
                        pi, rb, bc = tiles[ti]
                        gcol = rb * TOWN + bc
                        hb_ = ti % 2
                        hxt = hx[hb_]
                        hk = f"rw_hx{hb_}"
                        def hcol(rb_, col, n):
                            return hT_all[col // 512].ap()[rb_ * D:(rb_ + 1) * D, col % 512:col % 512 + n].rearrange("(c p) t -> p c t", p=128)
                        P.load(hxt[:, :, 1:RT + 1], hcol(rb, bc, RT), hk, w=[hk])
                        if ti == 0:
                            P.memset('pool', hxt[:, :, 0:1], 0.0, w=[hk])
                        else:
                            _, prb, pbc = tiles[ti - 1]
                            P.load(hxt[:, :, 0:1], hcol(prb, pbc + RT - 1, 1), hk + "h", w=[hk], slow=True)
                        if ti == ntl - 1:
                            P.memset('pool', hxt[:, :, RT + 1:RT + 2], 0.0, w=[hk])
                        else:
                            _, nrb, nbc = tiles[ti + 1]
                            P.load(hxt[:, :, RT + 1:RT + 2], hcol(nrb, nbc, 1), hk + "g", w=[hk], slow=True)
                        if d == 1:
                            for hp in range(4):
                                P.load(yfl[hp][:], yf_d.ap()[hp * 128:(hp + 1) * 128, gcol:gcol + RT], f"rw_yfl{hp}", w=[f"rw_yfl{hp}"])
                        P.tt('pool', xx[:], hxt[:, :, 0:RT], hxt[:, :, 2:RT + 2], ALU.add, [hk], ["rw_xx"])
                        P.stt('dve', xx[:], xx[:], 0.5, hxt[:, :, 1:RT + 1], ALU.mult, ALU.subtract, [hk, "rw_xx"], ["rw_xx"])
                        if SUB < 2:
                            continue
                        streams = [(0, 0, 'r'), (2, 512, 'k'), (3, 1024, 'v')]
                        if d == 1:
                            streams.append((5, 1536, 'g'))
                        streams += [(1, 2048, 'w'), (4, 2176, 'a')]
                        pcount = 0
                        for si, (ls, coff, nm) in enumerate(streams):
                            lb = si % 2
                            lk = f"rw_lp{lb}"
                            for c in range(8):
                                P.stt('dve', lerp[lb][:, c, :], xx[:, c, :], VMU(c, ls), hxt[:, c, 1:RT + 1],
                                      ALU.mult, ALU.add, [hk, "rw_xx", "rw_vec"], [lk])
                            if nm in "rkvg":
                                for hp in range(4):
                                    pb = psF[pcount % 2]; pk = f"psF{pcount % 2}"; pcount += 1
                                    for c in range(8):
                                        P.mm(pb[:, 0:RT], wbf[:, c, coff + hp * 128:coff + (hp + 1) * 128], lerp[lb][:, c, :],
                                             start=(c == 0), stop=(c == 7), r=["rw_wbf", lk], w=[pk])
                                    P.cp('act', pr[nm][hp][:], pb[:, 0:RT], [pk], [f"rw_p{nm}{hp}"])
                            else:
                                dirs = [d] if (nm == 'w' or d == 0) else [0, 1]
                                pb = psF[pcount % 2]; pk = f"psF{pcount % 2}"; pcount += 1
                                for c in range(8):
                                    P.mm(pb[:, 0:RT], wbf[:, c, coff:coff + 128], lerp[lb][:, c, :],
                                         start=(c == 0), stop=(c == 7), r=["rw_wbf", lk], w=[pk])
                                for dd in dirs:
                                    lo = dd * 64
                                    if nm == 'w':
                                        P.act(wlt[dd][lo:lo + 64, :], pb[lo:lo + 64, 0:RT], AF.Tanh, r=[pk], w=[f"rw_wl{dd}"])
                                    else:
                                        P.cp('act', alt[dd][lo:lo + 64, :], pb[lo:lo + 64, 0:RT], [pk], [f"rw_al{dd}"])
                        for hp in range(4 if SUB >= 3 else 0):
                            rk_, kk_, vk_, gk_ = (f"rw_p{s}{hp}" for s in "rkvg")
                            r_t, k_t, v_t, g_t = (pr[s][hp] for s in "rkvg")
                            csl = slice(hp * 128, (hp + 1) * 128)
                            ps2 = psF[2]
                            P.ts('dve', kk[:], k_t[:], VKK(hp), None, ALU.mult, r=[kk_, "rw_vec"], w=["rw_kk"])
                            P.tt('dve', kk2[:], kk[:], kk[:], ALU.mult, ["rw_kk"], ["rw_kk2"])
                            P.mm(ps2[:, 0:RT], BONES, kk2[:], r=["cst", "rw_kk2"], w=["psF2"])
                            P.act(kk2[:], ps2[:, 0:RT], AF.Sqrt, r=["psF2"], w=["rw_kk2"])
                            P.ts('dve', kk2[:], kk2[:], 1e-12, None, ALU.max, r=["rw_kk2"], w=["rw_kk2"])
                            P.op('dve', lambda h: h.reciprocal(kk2[:], kk2[:]), ["rw_kk2"], ["rw_kk2"])
                            P.tt('dve', kkn[:], kk[:], kk2[:], ALU.mult, ["rw_kk", "rw_kk2"], ["rw_kkn"])
                            lo = d * 64
                            P.mm(ps2[:, RT:2 * RT], w2bf[:, 0, csl], wlt[d][:, :], r=["rw_w2bf", f"rw_wl{d}"], w=["psF2"])
                            P.act(lw[:], ps2[:, RT:2 * RT], AF.Sigmoid, bias=VW0(d, hp), r=["psF2", "rw_vec"], w=["rw_lw"])
                            P.ts('pool', lw[:], lw[:], NEG_E05, None, ALU.mult, r=["rw_lw"], w=["rw_lw"])
                            adirs = [d] if d == 0 else [1, 0]
                            for dd in adirs:
                                lo2 = dd * 64
                                P.mm(ps2[:, 0:RT], a2bf[:, 0, csl], alt[dd][:, :], r=["rw_a2bf", f"rw_al{dd}"], w=["psF2"])
                                P.act(av[dd][:], ps2[:, 0:RT], AF.Sigmoid, bias=VA0(dd, hp), r=["psF2", "rw_vec"], w=[f"rw_a{dd}"])
                            P.ts('dve', tmp[:], av[d][:], VKA(hp), VOM(hp), ALU.mult, ALU.add, r=[f"rw_a{d}", "rw_vec"], w=["rw_tmp"])
                            P.tt('dve', kd[:], k_t[:], tmp[:], ALU.mult, [kk_, "rw_tmp"], ["rw_kd"])
                            P.tt('pool', bb[:], kkn[:], av[d][:], ALU.mult, ["rw_kkn", f"rw_a{d}"], ["rw_b"])
                            if d == 1:
                                P.tt('pool', q1[:], av[0][:], av[1][:], ALU.add, ["rw_a0", "rw_a1"], ["rw_q1"])
                                P.ts('pool', q1[:], q1[:], VKA(hp), VOM2(hp), ALU.mult, ALU.add, r=["rw_q1", "rw_vec"], w=["rw_q1"])
                                P.tt('pool', q1[:], q1[:], k_t[:], ALU.mult, ["rw_q1", kk_], ["rw_q1"])
                                P.stt('dve', q1[:], r_t[:], VRK(hp), q1[:], ALU.mult, ALU.mult, [rk_, "rw_vec", "rw_q1"], ["rw_q1"])
                                P.mm(ps2[:, RT:2 * RT], BONES, q1[:], r=["cst", "rw_q1"], w=["psF2"])
                                P.tt('dve', q3[:], ps2[:, RT:2 * RT], v_t[:], ALU.mult, ["psF2", vk_], ["rw_q3"])
                            for ci in range(2):
                                cs = slice(ci * CH, (ci + 1) * CH)
                                P.op('dve', lambda h, cs=cs: h.tensor_tensor_scan(cum[:, cs], ones[:, cs], lw[:, cs], 0.0, ALU.mult, ALU.add),
                                     ["rw_ones", "rw_lw"], ["rw_cum"])
                                if d == 0:
                                    P.cp('pool', gC[:, ci:ci + 1], cum[:, ci * CH + CH - 1:ci * CH + CH], ["rw_cum"], ["rw_gC"])
                                else:
                                    P.cp('pool', gC[:, ci:ci + 1], cum[:, ci * CH + CH - 1:ci * CH + CH], ["rw_cum"], ["rw_gC"])
                                    P.tt('dve', cum[:, cs], lw[:, cs], cum[:, cs], ALU.subtract, ["rw_lw", "rw_cum"], ["rw_cum"])
                                    P.ts('dve', cum[:, cs], cum[:, cs], gC[:, ci:ci + 1], None, ALU.add, r=["rw_cum", "rw_gC"], w=["rw_cum"])
                            krv = krt[:].rearrange("p c j t -> p c (j t)")
                            P.act(ex[:], cum[:], AF.Exp, r=["rw_cum"], w=["rw_ex"])
                            P.tt('dve', krt[:, :, 1, :], r_t[:].rearrange("p (c t) -> p c t", c=2), ex[:].rearrange("p (c t) -> p c t", c=2),
                                 ALU.mult, [rk_, "rw_ex"], ["rw_krt"])
                            P.tt('pool', tmp[:], cum[:], lw[:], ALU.subtract, ["rw_cum", "rw_lw"], ["rw_tmp"])
                            P.act(ex[:], tmp[:], AF.Exp, r=["rw_tmp"], w=["rw_ex"])
                            P.tt('dve', krt[:, :, 0, :], kkn[:].rearrange("p (c t) -> p c t", c=2), ex[:].rearrange("p (c t) -> p c t", c=2),
                                 ALU.mult, ["rw_kkn", "rw_ex"], ["rw_krt"])
                            for hz in range(2):
                                pz = slice(hz * 64, hz * 64 + 64)
                                P.tt('pool', kktZ[hz][pz, :, :], kkn[pz, :].rearrange("p (c t) -> p c t", c=2), ex[pz, :].rearrange("p (c t) -> p c t", c=2),
                                     ALU.mult, ["rw_kkn", "rw_ex"], [f"rw_kktZ{hz}"])
                            P.act(ex[:], cum[:], AF.Exp, scale=-1.0, r=["rw_cum"], w=["rw_ex"])
                            P.tt('pool', bhb[:], bb[:], ex[:], ALU.mult, ["rw_b", "rw_ex"], ["rw_bhb"])
                            for hz in range(2):
                                pz = slice(hz * 64, hz * 64 + 64)
                                P.tt('dve', khbZ[hz][pz, :], kd[pz, :], ex[pz, :], ALU.mult, ["rw_kd", "rw_ex"], [f"rw_khbZ{hz}"])
                                P.tt('pool', bhbZ[hz][pz, :], bb[pz, :], ex[pz, :], ALU.mult, ["rw_b", "rw_ex"], [f"rw_bhbZ{hz}"])
                            for ci in range(2):
                                cs = slice(ci * CH, (ci + 1) * CH)
                                P.act(ex[:, cs], cum[:, cs], AF.Exp, bias=gC[:, ci:ci + 1], scale=-1.0, r=["rw_cum", "rw_gC"], w=["rw_ex"])
                            P.tt('dve', Kbb[:], kd[:], ex[:], ALU.mult, ["rw_kd", "rw_ex"], ["rw_Kbb"])
                            P.tt('pool', Bbb[:], bb[:], ex[:], ALU.mult, ["rw_b", "rw_ex"], ["rw_Bbb"])
                            P.act(gC[:], gC[:], AF.Exp, r=["rw_gC"], w=["rw_gC"])
                            P.cp('pool', vbf[:], v_t[:], [vk_], ["rw_vbf"])
                            for ci in (([0, 1] if d == 0 else [1, 0]) if SUB >= 4 else []):
                                cs = slice(ci * CH, (ci + 1) * CH)
                                cb = chunk_ctr[0] % 2
                                chunk_ctr[0] += 1
                                tk = f"rw_tokm{cb}"
                                ptk = psB[0]
                                P.tr(ptk[:, 0:128], krt[:, ci, 0, :], IDENT, ["rw_krt", "idb"], ["psB0"])
                                P.tr(ptk[:, 128:256], Kbb[:, cs], IDENT, ["rw_Kbb", "idb"], ["psB0"])
                                P.tr(ptk[:, 256:384], Bbb[:, cs], IDENT, ["rw_Bbb", "idb"], ["psB0"])
                                P.tr(ptk[:, 384:512], vbf[:, cs], IDENT, ["rw_vbf", "idb"], ["psB0"])
                                P.cp('act', tokm[cb][:], ptk[:, 0:512].rearrange("p (a t) -> p a t", a=4), ["psB0"], [tk])
                                P.cp('pool', diag_ap(vpad[cb]), tokm[cb][:, 3, :].rearrange("p (h j) -> p h j", h=2), [tk], [f"rw_vpad{cb}"])
                                if SUB < 5:
                                    continue
                                for hh in range(2):
                                    pl = slice(hh * 64, hh * 64 + 64)
                                    pA = psF[3]
                                    P.mm(pA[:, 0:256], bhbZ[hh][:, cs], krv[:, ci, :], r=[f"rw_bhbZ{hh}", "rw_krt"], w=["psF3"])
                                    P.mm(pA[:, 256:512], khbZ[hh][:, cs], krv[:, ci, :], r=[f"rw_khbZ{hh}", "rw_krt"], w=["psF3"])
                                    p4 = psF[4]
                                    P.mm(p4[:, 0:128], kktZ[hh][:, ci, :], bhb[:, cs], r=[f"rw_kktZ{hh}", "rw_bhb"], w=["psF4"])
                                    if K5 < 2:
                                        continue
                                    P.tt('dve', AbT[hh][:], pA[:, 0:256], MA, ALU.mult, ["psF3", "cst"], [f"rw_AbT{hh}"])
                                    P.tt('dve', AkT[hh][:], pA[:, 256:512], MA, ALU.mult, ["psF3", "cst"], [f"rw_AkT{hh}"])
                                    if K5 < 3:
                                        continue
                                    P.stt('dve', Gp[0][:, 0, :], p4[:, 0:128], -1.0, MP, ALU.mult, ALU.mult, ["psF4", "cst"], ["rw_G0"])
                                    if K5 < 4:
                                        continue
                                    P.ts('pool', Gp[0][:, 1, :], AbT[hh][:, 0:128], -1.0, None, ALU.mult, r=[f"rw_AbT{hh}"], w=["rw_G0"])
                                    P.tt('pool', TTp[0][:], IDENT, AbT[hh][:, 0:128], ALU.subtract, ["idb", f"rw_AbT{hh}"], ["rw_TT0"])
                                    gi = 0
                                    ti_ = 0
                                    if SUB < 6:
                                        continue
                                    for lvl in range(1, 7):
                                        Gc = Gp[gi]; Gn = Gp[1 - gi]
                                        gck = f"rw_G{gi}"; gnk = f"rw_G{1 - gi}"
                                        P.mm(p4[:, 128:256], Gc[:, 1, :], Gc[:, 0, :], r=[gck], w=["psF4"])
                                        if lvl < 6:
                                            P.mm(p4[:, 256:384], Gc[:, 0, :], Gc[:, 1, :], r=[gck], w=["psF4"])
                                            P.cp('act', Gn[:], p4[:, 128:384].rearrange("p (a t) -> p a t", a=2), ["psF4"], [gnk])
                                        else:
                                            P.cp('act', Gn[:, 0, :], p4[:, 128:256], ["psF4"], [gnk])
                                        Tc = TTp[ti_]; tck = f"rw_TT{ti_}"
                                        if lvl < 6:
                                            Tn = TTp[1 - ti_]; tnk = f"rw_TT{1 - ti_}"
                                        else:
                                            Tn = TTh[hh]; tnk = f"rw_TTh{hh}"
                                        P.mm(psF[5][:, 0:128], Gn[:, 0, :], Tc[:], r=[gnk, tck], w=["psF5"])
                                        P.tt('dve', Tn[:], psF[5][:, 0:128], Tc[:], ALU.add, ["psF5", tck], [tnk])
                                        gi = 1 - gi
                                        ti_ = 1 - ti_
                                    if SUB < 7:
                                        continue
                                    p5 = psF[5]
                                    P.mm(p5[:, 128:256], tokm[cb][:, 0, :], TTh[hh][:], r=[tk, f"rw_TTh{hh}"], w=["psF5"])
                                    if K7 < 2:
                                        continue
                                    P.cp('act', WTb[pl, :], p5[pl, 128:256], ["psF5"], ["rw_WT"])
                                    if K7 < 3:
                                        continue
                                    P.mm(p5[:, 256 + hh * 64:320 + hh * 64], AkT[hh][:, 0:128], tokm[cb][:, 3, pl], r=[f"rw_AkT{hh}", tk], w=["psF5"])
                                    if K7 < 4:
                                        continue
                                    P.cp('act', AVb[:, hh, :], p5[:, 256 + hh * 64:320 + hh * 64], ["psF5"], ["rw_AV"])
                                if SUB < 8:
                                    continue
                                p5 = psF[6]
                                hk2 = f"rw_Hbf{hp}"
                                for hh in range(2):
                                    pl = slice(hh * 64, hh * 64 + 64)
                                    P.mm(p5[:, 0 + hh * 64:64 + hh * 64], TTh[hh][:], AVb[:, hh, :], start=True, stop=False,
                                         r=[f"rw_TTh{hh}", "rw_AV"], w=["psF6"])
                                    P.mm(p5[:, 0 + hh * 64:64 + hh * 64], WTb[:, :], Hbf[hp][:, pl], start=False, stop=True,
                                         r=["rw_WT", hk2], w=["psF6"])
                                if K8 < 2:
                                    continue
                                P.act(ucat[cb][:], p5[:, 0:128], AF.Identity, scale=-1.0, r=["psF6"], w=[f"rw_ucat{cb}"])
                                P.cp('pool', diag_ap(upad[cb]), ucat[cb][:].rearrange("p (h j) -> p h j", h=2), [f"rw_ucat{cb}"], [f"rw_upad{cb}"])
                                if K8 < 3:
                                    continue
                                pY = p5[:, 128:256]
                                P.mm(pY, Hbf[hp][:], krt[:, ci, 1, :], start=True, stop=False, r=[hk2, "rw_krt"], w=["psF6"])
                                for hh in range(2):
                                    P.mm(pY, vpad[cb][:, hh, :], AkT[hh][:, 128:256], start=False, stop=False,
                                         r=[f"rw_vpad{cb}", f"rw_AkT{hh}"], w=["psF6"])
                                    P.mm(pY, upad[cb][:, hh, :], AbT[hh][:, 128:256], start=False, stop=(hh == 1),
                                         r=[f"rw_upad{cb}", f"rw_AbT{hh}"], w=["psF6"])
                                if K8 < 4:
                                    continue
                                if d == 0:
                                    P.cp('act', yT[hp][:, cs], pY, ["psF6"], [f"rw_yT{hp}"])
                                else:
                                    P.tt('dve', yT[hp][:, cs], pY, yfl[hp][:, cs], ALU.add, ["psF6", f"rw_yfl{hp}"], [f"rw_yT{hp}"])
                                if K8 < 5:
                                    continue
                                pH = p5[:, 256:384]
                                P.mm(pH, tokm[cb][:, 1, :], tokm[cb][:, 3, :], start=True, stop=False, r=[tk], w=["psF6"])
                                P.mm(pH, tokm[cb][:, 2, :], ucat[cb][:], start=False, stop=True, r=[tk, f"rw_ucat{cb}"], w=["psF6"])
                                if K8 < 6:
                                    continue
                                P.tt('dve', htmp[:], pH, BONES, ALU.mult, ["psF6", "cst"], ["rw_htmp"])
                                P.stt('dve', H32[hp][:], H32[hp][:], gC[:, ci:ci + 1], htmp[:], ALU.mult, ALU.add,
                                      [f"rw_H32{hp}", "rw_gC", "rw_htmp"], [f"rw_H32{hp}"])
                                P.cp('act', Hbf[hp][:], H32[hp][:], [f"rw_H32{hp}"], [hk2])
                            if d == 0:
                                P.load(yf_d.ap()[hp * 128:(hp + 1) * 128, gcol:gcol + RT], yT[hp][:], f"rw_yfs{hp}",
                                       r=[f"rw_yT{hp}"], w=[], q='pool')
                            else:
                                ps2 = psF[2]
                                y_ = yT[hp]
                                P.mm(ps2[:, 0:RT], BONES, y_[:], r=["cst", f"rw_yT{hp}"], w=["psF2"])
                                P.stt('dve', q1[:], ps2[:, 0:RT], -1.0 / 64, y_[:], ALU.mult, ALU.add, ["psF2", f"rw_yT{hp}"], ["rw_q1"])
                                P.tt('dve', q2[:], q1[:], q1[:], ALU.mult, ["rw_q1"], ["rw_q2"])
                                P.mm(ps2[:, RT:2 * RT], BONES, q2[:], r=["cst", "rw_q2"], w=["psF2"])
                                P.ts('dve', q2[:], ps2[:, RT:2 * RT], 1.0 / 64, 64e-5, ALU.mult, ALU.add, r=["psF2"], w=["rw_q2"])
                                P.act(q2[:], q2[:], AF.Sqrt, r=["rw_q2"], w=["rw_q2"])
                                P.op('dve', lambda h: h.reciprocal(q2[:], q2[:]), ["rw_q2"], ["rw_q2"])
                                P.tt('dve', q1[:], q1[:], q2[:], ALU.mult, ["rw_q1", "rw_q2"], ["rw_q1"])
                                P.ts('dve', q1[:], q1[:], VLG(hp), VLB(hp), ALU.mult, ALU.add, r=["rw_q1", "rw_vec"], w=["rw_q1"])
                                P.tt('dve', q1[:], q1[:], q3[:], ALU.add, ["rw_q1", "rw_q3"], ["rw_q1"])
                                P.act(q2[:], g_t[:], AF.Silu, r=[gk_], w=["rw_q2"])
                                P.tt('dve', zb[:, hp, :], q1[:], q2[:], ALU.mult, ["rw_q1", "rw_q2"], ["rw_zb"])
                        if d == 1:
                            for s2 in range(RT // 128):
                                pob = po[s2 % 2]; pok = f"rw_po{s2 % 2}"
                                for half in range(2):
                                    pb = psF[half]; pk = f"psF{half}"
                                    for hp in range(4):
                                        P.mm(pb[:, 0:512], zb[:, hp, s2 * 128:(s2 + 1) * 128], wobf[:, hp, half * 512:(half + 1) * 512],
                                             start=(hp == 0), stop=(hp == 3), r=["rw_zb", "rw_wobf"], w=[pk])
                                    P.cp('act', pob[:, half * 512:(half + 1) * 512], pb[:, 0:512], [pk], [pok])
                                b0 = bc + s2 * 128
                                prow = rb * 512 + b0 % 512
                                P.load(part_d[b0 // 512].ap()[prow:prow + 128, :], pob[:], pok + "s", r=[pok], w=[], q='pool')
                return

            sweep(0)
            P.barrier()
            if DBG >= 3:
                sweep(1)
        P.barrier()

    def mla_layer(j):
        M = ML[j]
        P.barrier()
        with ExitStack() as es:
            sb = lambda n, s, d=F32: es.enter_context(nc.sbuf_tensor(uname(n), s, d))
            psF, psB = alloc_psum(es, 4, 2)
            stg = [sb(f"stg{i}", [128, 1024]) for i in range(2)]
            wA = sb("ma_wA", [128, 8, 704], BF16)
            uqr = sb("ma_uqr", [128, 3, 1024], BF16)
            qn_t = sb("ma_qn", [128, 384]); kvn_t = sb("ma_kvn", [128, 256])
            fill_w_bf16(wA, "ma_wA", M["inA"].ap(), 8, 704, stg)
            fill_w_bf16(uqr, "ma_uqr", M["uqr"].ap(), 3, 1024, stg)
            P.load(qn_t[:], M["qn"].ap(), "ma_qn", w=["ma_qn"])
            P.load(kvn_t[:], M["kvn"].ap(), "ma_kvn", w=["ma_kvn"])
            hTt = [sb(f"ma_hT{i}", [128, 8, 128], BF16) for i in range(2)]
            rp = [sb(f"ma_rp{i}", [128, 64]) for i in range(2)]
            cA = sb("ma_cA", [128, 704]); sq = sb("ma_sq", [128, 384]); st = sb("ma_st", [128, 8])
            nb = sb("ma_nb", [128, 768], BF16)
            t1 = sb("ma_t1", [128, 32]); t2 = sb("ma_t2", [128, 32]); kr = sb("ma_kr", [128, 64])
            nT = sb("ma_nT", [128, 6, 128], BF16)
            cqS = [sb(f"ma_cqS{i}", [128, 3, 512], BF16) for i in range(2)]
            ltS = [sb(f"ma_ltS{i}", [128, 3, 512], BF16) for i in range(2)]
            qpS = [sb(f"ma_qpS{i}", [128, 8, 512], BF16) for i in range(2)]
            qf = sb("ma_qf", [128, 1024]); qa = sb("ma_qa", [128, 512]); qb = sb("ma_qb", [128, 512])
            qr = sb("ma_qr", [128, 1024], BF16)

            def bc_ap(t, c0):
                base = t[:, c0:c0 + 32]
                return bass.AP(base.tensor, base.offset, [list(base.ap[0]), [0, 16], [1, 32]])

            for t in range(TOWN // 128):
                b = t % 2
                k4 = t // 4; sub = t % 4; g4 = k4 % 2
                hk = f"ma_hT{b}"
                P.load(hTt[b][:], hT_loc[k4].ap()[:, sub * 128:(sub + 1) * 128].rearrange("(c p) t -> p c t", p=128), hk, w=[hk])
                P.load(rp[b][:], rope_d.ap()[t * 128:(t + 1) * 128, :], f"ma_rp{b}", w=[f"ma_rp{b}"])
                for half in range(2):
                    for c in range(8):
                        P.mm(psF[half][:, 0:352], hTt[b][:, c, :], wA[:, c, half * 352:(half + 1) * 352],
                             start=(c == 0), stop=(c == 7), r=[hk, "ma_wA"], w=[f"psF{half}"])
                    P.cp('act', cA[:, half * 352:(half + 1) * 352], psF[half][:, 0:352], [f"psF{half}"], ["ma_cA"])
                P.memset('dve', st[:], 0.0, w=["ma_st"])
                for (c0, n, gtile, gk, o0, col) in ((0, 384, qn_t, "ma_qn", 0, 0), (384, 256, kvn_t, "ma_kvn", 384, 4)):
                    P.act(sq[:, 0:n], cA[:, c0:c0 + n], AF.Square, accum=st[:, col:col + 1], r=["ma_cA", "ma_st"], w=["ma_sq", "ma_st"])
                    P.ts('dve', st[:, col + 1:col + 2], st[:, col:col + 1], 1.0 / n, 1e-6, ALU.mult, ALU.add, r=["ma_st"], w=["ma_st"])
                    P.act(st[:, col + 2:col + 3], st[:, col + 1:col + 2], AF.Sqrt, r=["ma_st"], w=["ma_st"])
                    P.op('dve', lambda h, col=col: h.reciprocal(st[:, col + 3:col + 4], st[:, col + 2:col + 3]), ["ma_st"], ["ma_st"])
                    P.stt('dve', nb[:, o0:o0 + n], cA[:, c0:c0 + n], st[:, col + 3:col + 4], gtile[:], ALU.mult, ALU.mult,
                          ["ma_cA", "ma_st", gk], ["ma_nb"])
                cs_, sn_ = rp[b][:, 0:32], rp[b][:, 32:64]
                rk = f"ma_rp{b}"
                P.tt('dve', t1[:], cA[:, 640:672], cs_, ALU.mult, ["ma_cA", rk], ["ma_t1"])
                P.tt('dve', t2[:], cA[:, 672:704], sn_, ALU.mult, ["ma_cA", rk], ["ma_t2"])
                P.tt('dve', kr[:, 0:32], t1[:], t2[:], ALU.subtract, ["ma_t1", "ma_t2"], ["ma_kr"])
                P.tt('dve', t1[:], cA[:, 672:704], cs_, ALU.mult, ["ma_cA", rk], ["ma_t1"])
                P.tt('dve', t2[:], cA[:, 640:672], sn_, ALU.mult, ["ma_cA", rk], ["ma_t2"])
                P.tt('dve', kr[:, 32:64], t1[:], t2[:], ALU.add, ["ma_t1", "ma_t2"], ["ma_kr"])
                P.cp('dve', nb[:, 640:704], kr[:], ["ma_kr"], ["ma_nb"])
                P.cp('dve', nb[:, 704:768], kr[:], ["ma_kr"], ["ma_nb"])
                for c in range(6):
                    P.tr(psB[0][:, c * 128:(c + 1) * 128], nb[:, c * 128:(c + 1) * 128], IDENT, ["ma_nb", "idb"], ["psB0"])
                P.cp('act', nT[:], psB[0][:, 0:768].rearrange("p (c t) -> p c t", c=6), ["psB0"], ["ma_nT"])
                P.cp('pool', cqS[g4][:, :, sub * 128:(sub + 1) * 128], nT[:, 0:3, :], ["ma_nT"], [f"ma_cqS{g4}"])
                P.cp('pool', ltS[g4][:, :, sub * 128:(sub + 1) * 128], nT[:, 3:6, :], ["ma_nT"], [f"ma_ltS{g4}"])
                for half in range(2):
                    for c in range(3):
                        P.mm(psF[2 + half][:, 0:512], nT[:, c, :], uqr[:, c, half * 512:(half + 1) * 512],
                             start=(c == 0), stop=(c == 2), r=["ma_nT", "ma_uqr"], w=[f"psF{2 + half}"])
                    P.cp('act', qf[:, half * 512:(half + 1) * 512], psF[2 + half][:, 0:512], [f"psF{2 + half}"], ["ma_qf"])
                qv = qf[:].rearrange("p (h j) -> p h j", h=16)
                qrv = qr[:].rearrange("p (h j) -> p h j", h=16)
                qav = qa[:].rearrange("p (h j) -> p h j", h=16)
                qbv = qb[:].rearrange("p (h j) -> p h j", h=16)
                cB, sB = bc_ap(rp[b], 0), bc_ap(rp[b], 32)
                P.tt('dve', qav, qv[:, :, 0:32], cB, ALU.mult, ["ma_qf", rk], ["ma_qa"])
                P.tt('pool', qbv, qv[:, :, 32:64], sB, ALU.mult, ["ma_qf", rk], ["ma_qb"])
                P.tt('dve', qrv[:, :, 0:32], qav, qbv, ALU.subtract, ["ma_qa", "ma_qb"], ["ma_qr"])
                P.tt('dve', qav, qv[:, :, 32:64], cB, ALU.mult, ["ma_qf", rk], ["ma_qa"])
                P.tt('pool', qbv, qv[:, :, 0:32], sB, ALU.mult, ["ma_qf", rk], ["ma_qb"])
                P.tt('dve', qrv[:, :, 32:64], qav, qbv, ALU.add, ["ma_qa", "ma_qb"], ["ma_qr"])
                for c in range(8):
                    P.tr(psB[1][:, c * 128:(c + 1) * 128], qr[:, c * 128:(c + 1) * 128], IDENT, ["ma_qr", "idb"], ["psB1"])
                P.cp('act', qpS[g4][:, :, sub * 128:(sub + 1) * 128], psB[1][:].rearrange("p (c t) -> p c t", c=8), ["psB1"], [f"ma_qpS{g4}"])
                if sub == 3:
                    c0 = k4 * 512
                    P.load(cqT_d.ap()[:, c0:c0 + 512].rearrange("(c p) t -> p c t", p=128), cqS[g4][:], f"ma_cqs{g4}", r=[f"ma_cqS{g4}"], w=[], q='pool')
                    P.load(lat_loc[k4].ap().rearrange("(c p) t -> p c t", p=128), ltS[g4][:], f"ma_lts{g4}", r=[f"ma_ltS{g4}"], w=[], q='pool')
                    P.load(qpeT_d.ap()[:, c0:c0 + 512].rearrange("(c p) t -> p c t", p=128), qpS[g4][:], f"ma_qps{g4}", r=[f"ma_qpS{g4}"], w=[], q='pool')
        collective("AllGather", ALU.bypass, lat_loc, lat_all, f"agl{j}")
        with ExitStack() as es:
            sb = lambda n, s, d=F32: es.enter_context(nc.sbuf_tensor(uname(n), s, d))
            psF, psB = alloc_psum(es, 8, 0)
            stg = [sb(f"stg{i}", [128, 1024]) for i in range(2)]
            ukvk = sb("mt_ukvk", [128, 2, 2048], BF16); ukvv = sb("mt_ukvv", [128, 2, 2048], BF16)
            uqn = sb("mt_uqn", [128, 3, 2048], BF16)
            fill_w_bf16(ukvk, "mt_ukvk", M["ukvk"].ap(), 2, 2048, stg)
            fill_w_bf16(ukvv, "mt_ukvv", M["ukvv"].ap(), 2, 2048, stg)
            fill_w_bf16(uqn, "mt_uqn", M["uqn"].ap(), 3, 2048, stg)
            TKMAX = 4 * TSG
            kpe = sb("mt_kpe", [128, TKMAX], BF16)
            Kh = sb("mt_Kh", [128, TKMAX], BF16)
            Vh = sb("mt_Vh", [128, TKMAX // 128, 129], BF16)
            latp = [sb(f"mt_lp{i}", [128, 2, 512], BF16) for i in range(2)]
            cqt = [sb(f"mt_cq{i}", [128, 3, 512], BF16) for i in range(2)]
            qpt = [[sb(f"mt_qp{pp}{i}", [128, 512], BF16) for i in range(2)] for pp in range(2)]
            Qn = [sb(f"mt_Qn{i}", [128, 512], BF16) for i in range(2)]
            PT = [sb(f"mt_PT{i}", [128, 512], BF16) for i in range(3)]
            osb = [sb(f"mt_os{i}", [128, 4, 128], BF16) for i in range(2)]
            rden = sb("mt_rden", [128, 4])
            P.memset('pool', Vh[:, :, 128:129], 1.0, w=["mt_Vh"])
            for pp in range(2):
                for i in range(2):
                    P.memset('pool', qpt[pp][i][:], 0.0, w=[f"mt_qp{pp}{i}"])
            step = [0]
            for (Tk, q0, nq, pieces) in (
                (TPR, 0, TPR, [(lat_loc[p], 0, p * 512) for p in range(TPR // 512)]),
                (4 * TSG, TPR, TSG, [(lat_all[4 + (p % 8)], (p // 8) * 384, p * 512) for p in range(4 * TSG // 512)]),
            ):
                for (src, r0, kc) in pieces:
                    P.load(kpe[:, kc:kc + 512], src.ap()[r0 + 256:r0 + 384, :], "mt_kpe", w=["mt_kpe"])
                for h in range(16):
                    hc = slice(h * 128, (h + 1) * 128)
                    for pi, (src, r0, kc) in enumerate(pieces):
                        lb = pi % 2; lk = f"mt_lp{lb}"
                        P.load(latp[lb][:], src.ap()[r0:r0 + 256, :].rearrange("(c p) t -> p c t", p=128), lk, w=[lk])
                        for c in range(2):
                            P.mm(psF[6][:, 0:512], ukvk[:, c, hc], latp[lb][:, c, :], start=(c == 0), stop=(c == 1), r=["mt_ukvk", lk], w=["psF6"])
                        P.cp('dve', Kh[:, kc:kc + 512], psF[6][:, 0:512], ["psF6"], ["mt_Kh"])
                        for kt in range(4):
                            for c in range(2):
                                P.mm(psF[7][:, kt * 128:(kt + 1) * 128], latp[lb][:, c, kt * 128:(kt + 1) * 128], ukvv[:, c, hc],
                                     start=(c == 0), stop=(c == 1), r=[lk, "mt_ukvv"], w=["psF7"])
                        P.cp('act', Vh[:, kc // 128:kc // 128 + 4, 0:128], psF[7][:, 0:512].rearrange("p (a v) -> p a v", a=4), ["psF7"], ["mt_Vh"])
                    hl = slice((h % 2) * 64, (h % 2) * 64 + 64)
                    for qt in range(nq // 512):
                        qb_ = qt % 2
                        qc = q0 + qt * 512
                        P.load(cqt[qb_][:], cqT_d.ap()[:, qc:qc + 512].rearrange("(c p) t -> p c t", p=128), f"mt_cq{qb_}", w=[f"mt_cq{qb_}"])
                        par = h % 2
                        qpk = f"mt_qp{par}{qb_}"
                        P.load(qpt[par][qb_][hl, :], qpeT_d.ap()[(h // 2) * 128 + par * 64:(h // 2) * 128 + par * 64 + 64, qc:qc + 512], qpk, w=[qpk])
                        for c in range(3):
                            P.mm(psF[6][:, 0:512], uqn[:, c, hc], cqt[qb_][:, c, :], start=(c == 0), stop=(c == 2), r=["mt_uqn", f"mt_cq{qb_}"], w=["psF6"])
                        P.cp('dve', Qn[qb_][:], psF[6][:, 0:512], ["psF6"], [f"mt_Qn{qb_}"])
                        nkt = Tk // 128
                        base = step[0]
                        step[0] += nkt

                        def emit_st(kt):
                            sp = 4 + ((base + kt) % 2); spk = f"psF{sp}"
                            ks = slice(kt * 128, (kt + 1) * 128)
                            P.mm(psF[sp][:, 0:512], Kh[:, ks], Qn[qb_][:], start=True, stop=False, r=["mt_Kh", f"mt_Qn{qb_}"], w=[spk])
                            P.mm(psF[sp][:, 0:512], kpe[:, ks], qpt[par][qb_][:, :], start=False, stop=True, r=["mt_kpe", qpk], w=[spk])

                        emit_st(0)
                        for kt in range(nkt):
                            sp = 4 + ((base + kt) % 2); spk = f"psF{sp}"
                            pb3 = (base + kt) % 3
                            if kt + 1 < nkt:
                                emit_st(kt + 1)
                            P.act(PT[pb3][:], psF[sp][:, 0:512], AF.Exp, scale=QK_SCALE, r=[spk], w=[f"mt_PT{pb3}"])
                            for qs in range(4):
                                P.mm(psF[qs][:, 0:129], PT[pb3][:, qs * 128:(qs + 1) * 128], Vh[:, kt, :], start=(kt == 0), stop=(kt == nkt - 1),
                                     r=[f"mt_PT{pb3}", "mt_Vh"], w=[f"psF{qs}"])
                        ob = qt % 2
                        for qs in range(4):
                            P.op('dve', lambda h_, qs=qs: h_.reciprocal(rden[:, qs:qs + 1], psF[qs][:, 128:129]), [f"psF{qs}"], ["mt_rden"])
                            P.ts('dve', osb[ob][:, qs, :], psF[qs][:, 0:128], rden[:, qs:qs + 1], None, ALU.mult, r=[f"psF{qs}", "mt_rden"], w=[f"mt_os{ob}"])
                        P.load(o_scr.ap()[qc:qc + 512, hc].rearrange("(s p) v -> p s v", p=128), osb[ob][:], f"mt_oss{ob}", r=[f"mt_os{ob}"], w=[], q='pool')
        P.barrier()
        with ExitStack() as es:
            sb = lambda n, s, d=F32: es.enter_context(nc.sbuf_tensor(uname(n), s, d))
            psF, psB = alloc_psum(es, 6, 2)
            stg = [sb(f"stg{i}", [128, 1024]) for i in range(2)]
            wG = sb("me_wG", [128, 8, E], BF16); wO = sb("me_wO", [128, 16, D], BF16)
            fill_w_bf16(wG, "me_wG", M["inG"].ap(), 8, E, stg)
            fill_w_bf16(wO, "me_wO", M["out"].ap(), 16, D, stg)
            hTt = [sb(f"me_hT{i}", [128, 8, 128], BF16) for i in range(2)]
            ot = [sb(f"me_o{i}", [128, E], BF16) for i in range(2)]
            xt = [sb(f"me_x{i}", [128, D]) for i in range(2)]
            sg = sb("me_sg", [128, E]); zt = sb("me_z", [128, E], BF16); zT = sb("me_zT", [128, 16, 128], BF16)
            for t in range(TOWN // 128):
                b = t % 2
                rows = slice(t * 128, (t + 1) * 128)
                P.load(hTt[b][:], hT_loc[t // 4].ap()[:, (t % 4) * 128:(t % 4 + 1) * 128].rearrange("(c p) t -> p c t", p=128), f"me_hT{b}", w=[f"me_hT{b}"])
                P.load(ot[b][:], o_scr.ap()[rows, :], f"me_o{b}", w=[f"me_o{b}"])
                P.load(xt[b][:], x_own.ap()[rows, :], f"me_x{b}", w=[f"me_x{b}"])
                for q in range(4):
                    for c in range(8):
                        P.mm(psF[q][:, 0:512], hTt[b][:, c, :], wG[:, c, q * 512:(q + 1) * 512], start=(c == 0), stop=(c == 7),
                             r=[f"me_hT{b}", "me_wG"], w=[f"psF{q}"])
                    P.act(sg[:, q * 512:(q + 1) * 512], psF[q][:, 0:512], AF.Silu, r=[f"psF{q}"], w=["me_sg"])
                P.tt('dve', zt[:], sg[:], ot[b][:], ALU.mult, ["me_sg", f"me_o{b}"], ["me_z"])
                for hf in range(2):
                    for c in range(8):
                        e = hf * 8 + c
                        P.tr(psB[hf][:, c * 128:(c + 1) * 128], zt[:, e * 128:(e + 1) * 128], IDENT, ["me_z", "idb"], [f"psB{hf}"])
                    P.cp('act' if hf == 0 else 'dve', zT[:, hf * 8:(hf + 1) * 8, :], psB[hf][:].rearrange("p (c t) -> p c t", c=8), [f"psB{hf}"], ["me_zT"])
                for half in range(2):
                    for e in range(16):
                        P.mm(psF[4 + half][:, 0:512], zT[:, e, :], wO[:, e, half * 512:(half + 1) * 512], start=(e == 0), stop=(e == 15),
                             r=["me_zT", "me_wO"], w=[f"psF{4 + half}"])
                    P.tt('dve', xt[b][:, half * 512:(half + 1) * 512], xt[b][:, half * 512:(half + 1) * 512], psF[4 + half][:, 0:512], ALU.add,
                         [f"me_x{b}", f"psF{4 + half}"], [f"me_x{b}"])
                P.load(x_own.ap()[rows, :], xt[b][:], f"me_xs{b}", r=[f"me_x{b}"], w=[], q='pool')
        P.barrier()

    xcur = xin
    pend_delta = None
    for li in range(nlayers):
        jj = li // 2
        if li % 2 == 0:
            norm_phase(xcur, None, None, li)
            if DBG >= 1:
                collective("AllGather", ALU.bypass, hT_loc, hT_all, f"agh{li}")
            if DBG >= 2:
                rwkv_layer(jj)
            if DBG >= 4:
                collective("ReduceScatter", ALU.add, part_d, delta_d, f"rs{li}")
                pend_delta = delta_d
        else:
            norm_phase(xcur, pend_delta, x_own, li)
            pend_delta = None
            xcur = x_own
            mla_layer(jj)
    if pend_delta is not None:
        norm_phase(xcur, pend_delta, x_own, 4, final=True)
    else:
        norm_phase(xcur, None, None, 4, final=True)
    P.emit()
    return nc


def _consts():
    c = np.zeros((128, 768), np.float32)
    i = np.arange(128)[:, None]
    t = np.arange(128)[None, :]
    c[:, 0:128] = (i == t)
    c[:, 128:256] = (i < t)
    c[:, 256:384] = (i <= t)
    c[:, 384:512] = (i > t)
    c[:, 512:640] = (i >= t)
    c[:, 640:768] = ((i // 64) == (t // 64))
    return c


def _rope_table(pos):
    inv_freq = (1.0 / (np.float32(10000.0) ** (np.arange(0, 64, 2, dtype=np.float32) / np.float32(64)))).astype(np.float32)
    ang = (pos.astype(np.float32)[:, None] * inv_freq[None, :]).astype(np.float32)
    return np.concatenate([np.cos(ang), np.sin(ang)], axis=1).astype(np.float32)


_NC_CACHE = {}


def make_in_maps(inp):
    f = lambda a: np.ascontiguousarray(np.asarray(a, dtype=np.float32))
    xp, xs = f(inp["x_prompt"]), f(inp["x_sample"])
    consts = _consts()
    lng = np.concatenate([f(inp["ln_g"]), f(inp["final_g"])[None, :]], axis=0)
    lng = np.ascontiguousarray(np.broadcast_to(lng[:, None, :], (5, 128, D)))
    rw_in, rw_out = f(inp["rw_in"]), f(inp["rw_out"])
    maps = []
    for c in range(8):
        g, r = c // 4, c % 4
        m = {}
        m["xin"] = np.ascontiguousarray(np.concatenate([xp[c], xs[g, r * TSG:(r + 1) * TSG]], axis=0))
        m["consts"] = consts
        pos = np.concatenate([np.arange(TPR), np.arange(r * TSG, (r + 1) * TSG)])
        m["rope"] = _rope_table(pos)
        m["lng"] = lng
        my = slice(r * 512, (r + 1) * 512)
        for j in range(2):
            w = rw_in[j]
            cols = [w[:, s * E:(s + 1) * E][:, my] for s in range(4)] + [w[:, 4 * E:4 * E + 128], w[:, 4 * E + 128:4 * E + 256]]
            m[f"rw_wloc{j}"] = np.ascontiguousarray(np.concatenate(cols, axis=1))
            m[f"rw_w2{j}"] = np.ascontiguousarray(f(inp["rw_w2"])[j].reshape(128, E)[:, my])
            m[f"rw_a2{j}"] = np.ascontiguousarray(f(inp["rw_a2"])[j].reshape(128, E)[:, my])
            m[f"rw_wout{j}"] = np.ascontiguousarray(rw_out[j][my, :])
            vec = np.zeros((128, NVEC), np.float32)
            mu = f(inp["rw_mu"])[j]
            vec[:, 0:48] = mu.reshape(6, 8, 128).transpose(2, 1, 0).reshape(128, 48)
            def pc(v):
                return v[my].reshape(4, 128).T
            vec[:, 48:52] = pc(f(inp["rw_w0"])[j, 0]); vec[:, 52:56] = pc(f(inp["rw_w0"])[j, 1])
            vec[:, 56:60] = pc(f(inp["rw_a0"])[j, 0]); vec[:, 60:64] = pc(f(inp["rw_a0"])[j, 1])
            vec[:, 64:68] = pc(f(inp["rw_kk"])[j]); vec[:, 68:72] = pc(f(inp["rw_ka"])[j])
            vec[:, 72:76] = pc(f(inp["rw_rk"])[j].reshape(E))
            vec[:, 76:80] = pc(f(inp["rw_lnx_g"])[j]); vec[:, 80:84] = pc(f(inp["rw_lnx_b"])[j])
            m[f"rw_vec{j}"] = vec
            mi = f(inp["ml_in"])[j]
            m[f"ml_inA{j}"] = np.ascontiguousarray(mi[:, 0:704])
            m[f"ml_inG{j}"] = np.ascontiguousarray(mi[:, 704:704 + E])
            uq = f(inp["ml_uq"])[j].reshape(384, 16, 192)
            m[f"ml_uqn{j}"] = np.ascontiguousarray(uq[:, :, 0:128].reshape(384, 2048))
            m[f"ml_uqr{j}"] = np.ascontiguousarray(uq[:, :, 128:192].reshape(384, 1024))
            ukv = f(inp["ml_ukv"])[j].reshape(256, 16, 256)
            m[f"ml_ukvk{j}"] = np.ascontiguousarray(ukv[:, :, 0:128].reshape(256, 2048))
            m[f"ml_ukvv{j}"] = np.ascontiguousarray(ukv[:, :, 128:256].reshape(256, 2048))
            m[f"ml_out{j}"] = f(inp["ml_out"])[j]
            m[f"ml_qn{j}"] = np.ascontiguousarray(np.broadcast_to(f(inp["ml_qn"])[j][None, :], (128, 384)))
            m[f"ml_kvn{j}"] = np.ascontiguousarray(np.broadcast_to(f(inp["ml_kvn"])[j][None, :], (128, 256)))
        maps.append(m)
    return maps


def run(inp, nlayers=4, trace=False):
    if nlayers not in _NC_CACHE:
        _NC_CACHE[nlayers] = build(nlayers)
    nc = _NC_CACHE[nlayers]
    maps = make_in_maps(inp)
    res = run_bass_kernel_spmd(nc, maps, core_ids=list(range(8)), trace=trace) if trace else \
        run_bass_kernel_spmd(nc, maps, core_ids=list(range(8)))
    yp = np.zeros((8, TPR, D), np.float32)
    ys = np.zeros((2, 4 * TSG, D), np.float32)
    for c in range(8):
        y = np.asarray(res.results[c]["y"], dtype=np.float32)
        yp[c] = y[0:TPR]
        ys[c // 4, (c % 4) * TSG:(c % 4 + 1) * TSG] = y[TPR:]
    return yp, ys, res


def kernel(**inputs):
    yp, ys, _ = run(inputs, 4)
    return (yp, ys)
```
